# Optimizing a Trainium2 kernel written in Bass

```python
import jax, jax.numpy as jnp
from jax import lax
import numpy as np

D_MODEL = 1024
BATCH = 4
SEQ = 4096
DEPTH = 2

N_EVEN = (DEPTH + 1) // 2
N_ODD = DEPTH // 2

LRU_WIDTH = D_MODEL // 2
LRU_HEADS = 8
LRU_HEAD_DIM = LRU_WIDTH // LRU_HEADS
LRU_CONV = 4
LRU_C = 8.0
LRU_MIN_RAD = 0.9
LRU_MAX_RAD = 0.999
SC_WIDTH = D_MODEL // 2
SC_CONV = 3
EVEN_IN = 2 * LRU_WIDTH + 3 * SC_WIDTH

SGU_WIDTH = D_MODEL // 2
SGU_HEADS = 8
SGU_HEAD_DIM = SGU_WIDTH // SGU_HEADS
CHUNK = 128
FOX_HEADS = 8
FOX_HEAD_DIM = 64
FOX_WIDTH = FOX_HEADS * FOX_HEAD_DIM
Q_BLOCK = 128
ODD_IN = 2 * SGU_WIDTH + 3 * FOX_WIDTH + FOX_HEADS

D_FF = 2816
FFN_CONV = 3
EPS = 1e-6

kernel_name = "hybrid_rglru_shortconv_sgu_fox_block"


def rmsnorm(x, g):
    xf = x.astype(jnp.float32)
    y = xf * lax.rsqrt(jnp.mean(xf * xf, axis=-1, keepdims=True) + EPS)
    return (y * g.astype(jnp.float32)).astype(x.dtype)


def causal_depthwise_conv(x, w, b):
    k_width, ch = w.shape
    out = lax.conv_general_dilated(
        x, w[:, None, :].astype(x.dtype), window_strides=(1,),
        padding=[(k_width - 1, 0)], dimension_numbers=('NWC', 'WIO', 'NWC'),
        feature_group_count=ch)
    return out + b.astype(x.dtype)


def rg_lru(x, w_a, b_a, w_x, b_x, lam):
    bsz, s, w = x.shape
    xh = x.reshape(bsz, s, LRU_HEADS, LRU_HEAD_DIM)
    r = jax.nn.sigmoid((jnp.einsum('bshi,hij->bshj', xh, w_a).reshape(bsz, s, w) + b_a).astype(jnp.float32))
    i = jax.nn.sigmoid((jnp.einsum('bshi,hij->bshj', xh, w_x).reshape(bsz, s, w) + b_x).astype(jnp.float32))
    log_a = -LRU_C * r * jax.nn.softplus(-lam.astype(jnp.float32))
    a = jnp.exp(log_a)
    u = jnp.sqrt(-jnp.expm1(2.0 * log_a)) * (i * x.astype(jnp.float32))

    def combine(left, right):
        a_l, h_l = left
        a_r, h_r = right
        return a_l * a_r, a_r * h_l + h_r

    _, h = lax.associative_scan(combine, (a, u), axis=1)
    return h.astype(x.dtype)


def even_mixer(x, w_in, lru_conv_w, lru_conv_b, lru_wa, lru_ba, lru_wx, lru_bx, lru_lambda,
               sconv_w, sconv_b, w_out):
    p = x @ w_in
    W, C = LRU_WIDTH, SC_WIDTH
    xa, ga, c_pre, b_post, vb = jnp.split(p, [W, 2 * W, 2 * W + C, 2 * W + 2 * C], axis=-1)
    xa = causal_depthwise_conv(xa, lru_conv_w, lru_conv_b)
    ya = rg_lru(xa, lru_wa, lru_ba, lru_wx, lru_bx, lru_lambda) * jax.nn.gelu(ga)
    yb = b_post * causal_depthwise_conv(c_pre * vb, sconv_w, sconv_b)
    return jnp.concatenate([ya, yb], axis=-1) @ w_out


def chunked_spatial_gating(u, g, g_norm, w_s, b_s):
    bsz, s, _ = u.shape
    n_chunks = s // CHUNK
    gv = rmsnorm(g.reshape(bsz, s, SGU_HEADS, SGU_HEAD_DIM), g_norm.reshape(SGU_HEADS, SGU_HEAD_DIM))
    gv = gv.reshape(bsz, n_chunks, CHUNK, SGU_HEADS, SGU_HEAD_DIM)
    w_causal = jnp.tril(w_s)
    mixed = jnp.einsum('gts,bnsgc->bntgc', w_causal, gv) + b_s.T[:, :, None]
    return u * mixed.reshape(bsz, s, SGU_WIDTH)


def forgetting_attention(q, k, v, f_logit, b_f):
    bsz, s, _ = q.shape
    nb = s // Q_BLOCK

    def heads(t):
        return t.reshape(bsz, s, FOX_HEADS, FOX_HEAD_DIM).transpose(0, 2, 1, 3)

    q, k, v = heads(q), heads(k), heads(v)
    log_f = jax.nn.log_sigmoid(f_logit.astype(jnp.float32) + b_f.astype(jnp.float32))
    c = jnp.cumsum(log_f, axis=1).transpose(0, 2, 1)
    q_blocks = q.reshape(bsz, FOX_HEADS, nb, Q_BLOCK, FOX_HEAD_DIM).transpose(2, 0, 1, 3, 4)
    c_blocks = c.reshape(bsz, FOX_HEADS, nb, Q_BLOCK).transpose(2, 0, 1, 3)
    starts = jnp.arange(nb) * Q_BLOCK
    key_pos = jnp.arange(s)
    scale = FOX_HEAD_DIM ** -0.5

    def block(args):
        qb, cb, start = args
        logits = (jnp.einsum('bhqd,bhkd->bhqk', qb, k).astype(jnp.float32) * scale
                  + cb[..., None] - c[:, :, None, :])
        q_pos = start + jnp.arange(Q_BLOCK)
        logits = jnp.where(key_pos[None, :] <= q_pos[:, None], logits, -jnp.inf)
        p = jax.nn.softmax(logits, axis=-1)
        return jnp.einsum('bhqk,bhkd->bhqd', p.astype(v.dtype), v)

    out = lax.map(block, (q_blocks, c_blocks, starts))
    return out.transpose(1, 0, 3, 2, 4).reshape(bsz, s, FOX_WIDTH)


def odd_mixer(x, w_in, sgu_norm, sgu_w, sgu_b, fox_bf, w_out):
    p = x @ w_in
    Z, F = 2 * SGU_WIDTH, FOX_WIDTH
    z, q, k, v, f = jnp.split(p, [Z, Z + F, Z + 2 * F, Z + 3 * F], axis=-1)
    z = jax.nn.gelu(z)
    u, g = jnp.split(z, 2, axis=-1)
    yc = chunked_spatial_gating(u, g, sgu_norm, sgu_w, sgu_b)
    yd = forgetting_attention(q, k, v, f, fox_bf)
    return jnp.concatenate([yc, yd], axis=-1) @ w_out


def conv_glu_ffn(x, w_up, conv_w, conv_b, w_down):
    h = causal_depthwise_conv(x @ w_up, conv_w, conv_b)
    gate, val = jnp.split(h, 2, axis=-1)
    return (jax.nn.silu(gate) * val) @ w_down


def setup_inputs(seed: int = 0) -> dict:
    key = jax.random.key(seed)
    ks = jax.random.split(key, 32)
    f32 = jnp.float32

    def dense(k, shape, fan_in):
        return jax.random.normal(k, shape, f32) * fan_in ** -0.5

    def gain(k, shape):
        return 1.0 + 0.05 * jax.random.normal(k, shape, f32)

    def bias(k, shape):
        return 0.1 * jax.random.normal(k, shape, f32)

    a_c = jax.random.uniform(ks[9], (N_EVEN, LRU_WIDTH), f32, LRU_MIN_RAD, LRU_MAX_RAD)
    s_base = a_c ** (1.0 / LRU_C)
    lru_lambda = jnp.log(s_base) - jnp.log1p(-s_base)

    return {
        "x": jax.random.normal(ks[0], (BATCH, SEQ, D_MODEL), f32),
        "mix0_norm": gain(ks[1], (N_EVEN, D_MODEL)),
        "mix0_w_in": dense(ks[2], (N_EVEN, D_MODEL, EVEN_IN), D_MODEL),
        "lru_conv_w": dense(ks[3], (N_EVEN, LRU_CONV, LRU_WIDTH), LRU_CONV),
        "lru_conv_b": bias(ks[4], (N_EVEN, LRU_WIDTH)),
        "lru_wa": dense(ks[5], (N_EVEN, LRU_HEADS, LRU_HEAD_DIM, LRU_HEAD_DIM), LRU_HEAD_DIM),
        "lru_ba": bias(ks[6], (N_EVEN, LRU_WIDTH)),
        "lru_wx": dense(ks[7], (N_EVEN, LRU_HEADS, LRU_HEAD_DIM, LRU_HEAD_DIM), LRU_HEAD_DIM),
        "lru_bx": bias(ks[8], (N_EVEN, LRU_WIDTH)),
        "lru_lambda": lru_lambda,
        "sconv_w": dense(ks[10], (N_EVEN, SC_CONV, SC_WIDTH), SC_CONV),
        "sconv_b": bias(ks[11], (N_EVEN, SC_WIDTH)),
        "mix0_w_out": dense(ks[12], (N_EVEN, LRU_WIDTH + SC_WIDTH, D_MODEL), LRU_WIDTH + SC_WIDTH),
        "mix1_norm": gain(ks[13], (N_ODD, D_MODEL)),
        "mix1_w_in": dense(ks[14], (N_ODD, D_MODEL, ODD_IN), D_MODEL),
        "sgu_norm": gain(ks[15], (N_ODD, SGU_WIDTH)),
        "sgu_w": dense(ks[16], (N_ODD, SGU_HEADS, CHUNK, CHUNK), CHUNK),
        "sgu_b": 1.0 + bias(ks[17], (N_ODD, SGU_HEADS, CHUNK)),
        "fox_bf": bias(ks[18], (N_ODD, FOX_HEADS)),
        "mix1_w_out": dense(ks[19], (N_ODD, SGU_WIDTH + FOX_WIDTH, D_MODEL), SGU_WIDTH + FOX_WIDTH),
        "ffn_norm": gain(ks[20], (DEPTH, D_MODEL)),
        "ffn_up": dense(ks[21], (DEPTH, D_MODEL, 2 * D_FF), D_MODEL),
        "ffn_conv_w": dense(ks[22], (DEPTH, FFN_CONV, 2 * D_FF), FFN_CONV),
        "ffn_conv_b": bias(ks[23], (DEPTH, 2 * D_FF)),
        "ffn_down": dense(ks[24], (DEPTH, D_FF, D_MODEL), D_FF),
        "final_norm": gain(ks[25], (D_MODEL,)),
    }


def reference(x, mix0_norm, mix0_w_in, lru_conv_w, lru_conv_b, lru_wa, lru_ba, lru_wx, lru_bx,
              lru_lambda, sconv_w, sconv_b, mix0_w_out, mix1_norm, mix1_w_in, sgu_norm, sgu_w,
              sgu_b, fox_bf, mix1_w_out, ffn_norm, ffn_up, ffn_conv_w, ffn_conv_b, ffn_down,
              final_norm):
    h = x
    for layer in range(DEPTH):
        i = layer // 2
        if layer % 2 == 0:
            h = h + even_mixer(rmsnorm(h, mix0_norm[i]), mix0_w_in[i], lru_conv_w[i], lru_conv_b[i],
                               lru_wa[i], lru_ba[i], lru_wx[i], lru_bx[i], lru_lambda[i],
                               sconv_w[i], sconv_b[i], mix0_w_out[i])
        else:
            h = h + odd_mixer(rmsnorm(h, mix1_norm[i]), mix1_w_in[i], sgu_norm[i], sgu_w[i],
                              sgu_b[i], fox_bf[i], mix1_w_out[i])
        h = h + conv_glu_ffn(rmsnorm(h, ffn_norm[layer]), ffn_up[layer], ffn_conv_w[layer],
                             ffn_conv_b[layer], ffn_down[layer])
    return rmsnorm(h, final_norm)
```

```python
import math
from contextlib import ExitStack

import numpy as np
import concourse.bass as bass
import concourse.mybir as mybir
from concourse.bass_utils import run_bass_kernel_spmd

F32 = mybir.dt.float32
BF16 = mybir.dt.bfloat16
AF = mybir.ActivationFunctionType
ALU = mybir.AluOpType
AX = mybir.AxisListType

PE, ACT, DVE, POOL, SP = "pe", "act", "dve", "pool", "sp"
ENGS = [PE, ACT, DVE, POOL, SP]

SEG = 1024
TT = 512
NT = SEG // TT
D = 1024
KC = 8
DFF = 2816
NJ = 22
EPS = 1e-6
NV = 448
NSLOT = 4
NEG = -30000.0

C_N0, C_N1, C_NF0, C_NF1, C_NFIN = 0, 8, 16, 24, 32
C_LCW, C_LCB, C_BA, C_BX, C_LAM = 40, 56, 60, 64, 68
C_SCW, C_SCB, C_GN = 72, 84, 88
C_FCW, C_FCB, C_BF = 92, 356, 444


class Tok:
    __slots__ = ("w", "rs")

    def __init__(self):
        self.w = None
        self.rs = []


class Ev:
    __slots__ = ("eng", "val", "needed", "is_dma", "semkey")

    def __init__(self, eng):
        self.eng = eng
        self.val = 0
        self.needed = False
        self.is_dma = False
        self.semkey = None


class Sched:
    def __init__(self):
        self.ops = {e: [] for e in ENGS}
        self.dma_cnt = {}
        self.burst = None

    def burst_begin(self):
        self.burst = []

    def burst_end(self):
        last = {}
        for ev in self.burst:
            last[ev.semkey] = max(last.get(ev.semkey, 0), ev.val)
        for ev in self.burst:
            ev.val = last[ev.semkey]
        self.burst = None

    def op(self, eng, fn, reads=(), writes=(), dma=None):
        ev = Ev(eng)
        if dma is not None:
            ev.is_dma = True
            ev.semkey = dma
            self.dma_cnt[dma] = self.dma_cnt.get(dma, 0) + 16
            ev.val = self.dma_cnt[dma]
            if self.burst is not None:
                self.burst.append(ev)
        deps = {}

        def add(d):
            if d is None:
                return
            if eng == PE and d.eng == PE and not d.is_dma:
                return
            if self.burst is not None and d in self.burst:
                return
            deps[id(d)] = d

        for t in reads:
            add(t.w)
        for t in writes:
            add(t.w)
            for r in t.rs:
                add(r)
        for d in deps.values():
            d.needed = True
        for t in reads:
            if not ev.is_dma:
                t.rs = [r for r in t.rs if r.is_dma or r.eng != eng]
            t.rs.append(ev)
        for t in writes:
            t.w = ev
            t.rs = []
        self.ops[eng].append((fn, list(deps.values()), ev))
        return ev


class Mem:
    PAGE = 256

    def __init__(self, sb, nwords):
        self.sb = sb
        self.n = nwords
        self.toks = [Tok() for _ in range((nwords + self.PAGE - 1) // self.PAGE)]
        self.ptr = 0
        self.hi = 0

    def alloc(self, dtype, shape, page_align=False):
        esz = 4 if dtype == F32 else 2
        n = int(np.prod(shape))
        words = (n * esz + 3) // 4
        words = (words + 7) // 8 * 8
        if page_align:
            self.ptr = (self.ptr + self.PAGE - 1) // self.PAGE * self.PAGE
        off = self.ptr
        self.ptr += words
        self.hi = max(self.hi, self.ptr)
        assert self.ptr <= self.n, f"SBUF overflow {self.ptr} > {self.n}"
        return Buf(self, off, dtype, tuple(shape))


class Buf:
    def __init__(self, mem, off, dtype, shape):
        self.mem = mem
        self.off = off
        self.dtype = dtype
        self.shape = shape
        self.esz = 4 if dtype == F32 else 2
        self.n = int(np.prod(shape))
        words = (self.n * self.esz + 3) // 4
        base = mem.sb[:, off:off + words]
        flat = base if dtype == F32 else base.bitcast(dtype)
        self.flat = flat[:, 0:self.n]
        if len(shape) == 1:
            self.v = self.flat
        elif len(shape) == 2:
            self.v = self.flat.rearrange("p (a b) -> p a b", a=shape[0])
        else:
            self.v = self.flat.rearrange("p (a b c) -> p a b c", a=shape[0], b=shape[1])

    def tk(self, lo=0, hi=None):
        if hi is None:
            hi = self.n
        b0 = self.off * 4 + lo * self.esz
        b1 = self.off * 4 + hi * self.esz
        return self.mem.toks[b0 // 1024:(b1 - 1) // 1024 + 1]

    def r(self, lo=0, hi=None, p0=0, p1=128):
        if hi is None:
            hi = self.n
        return (self.flat[p0:p1, lo:hi], self.tk(lo, hi))

    def c(self, a, lo=0, hi=None, p0=0, p1=128):
        B = self.shape[-1] if len(self.shape) == 2 else self.shape[1] * self.shape[2]
        if hi is None:
            hi = B
        return (self.flat[p0:p1, a * B + lo:a * B + hi], self.tk(a * B + lo, a * B + hi))


class Prog:
    def __init__(self, nseg=4, nlayers=2, debug=False):
        self.nseg = nseg
        self.nlayers = nlayers
        self.S = Sched()
        self.nc = bass.Bass("TRN2", target_bir_lowering=False)
        nc = self.nc
        self.T = nseg * SEG
        T = self.T
        dt = nc.dram_tensor
        self.d_x = dt("xT", [D, T], F32, kind="ExternalInput").ap()
        self.d_win0 = dt("w_in0", [D, 2560], F32, kind="ExternalInput").ap()
        self.d_wout0 = dt("w_out0", [D, D], F32, kind="ExternalInput").ap()
        self.d_win1 = dt("w_in1", [D, 2568], F32, kind="ExternalInput").ap()
        self.d_wout1 = dt("w_out1", [D, D], F32, kind="ExternalInput").ap()
        self.d_up = dt("w_up", [2, D, 2 * DFF], F32, kind="ExternalInput").ap()
        self.d_down = dt("w_down", [2, DFF, D], F32, kind="ExternalInput").ap()
        self.d_wbd = dt("wbd", [128, 8, 128], F32, kind="ExternalInput").ap()
        self.d_wst = dt("wst", [128, 8, 128], F32, kind="ExternalInput").ap()
        self.d_wf = dt("wf", [128, 8, 8], F32, kind="ExternalInput").ap()
        self.d_vecs = dt("vecs", [128, NV], F32, kind="ExternalInput").ap()
        self.d_bs = dt("bs", [128, 4, 128], F32, kind="ExternalInput").ap()
        self.d_cst = dt("cst", [128, 3, 128], F32, kind="ExternalInput").ap()
        self.d_sel = dt("sel", [8, 8, 128], F32, kind="ExternalInput").ap()
        self.d_y = dt("yT", [D, T], F32, kind="ExternalOutput").ap()
        self.d_kt = dt("kt_scratch", [4, 128, T], BF16).ap()
        self.d_va = dt("va_scratch", [4, 128, T // 128, 192], BF16).ap()
        self.tok_kt = [Tok() for _ in range(4)]
        self.tok_va = Tok()
        self.out_evs = []
        self.ps_tok = [Tok() for _ in range(8)]
        self.ps_i = 0
        self.slot_i = 0
        self.rot = {}

    def bank(self, lo=0, hi=8):
        key = (lo, hi)
        i = self.rot.get(key, 0)
        self.rot[key] = i + 1
        b = lo + i % (hi - lo)
        return b

    def pv(self, b, lo=0, hi=TT, p0=0, p1=128):
        return (self.PS[p0:p1, b, lo:hi], [self.ps_tok[b]])

    def op(self, eng, fn, reads, writes, dma=None):
        r = []
        for x in reads:
            r += x[1] if isinstance(x, tuple) else x
        w = []
        for x in writes:
            w += x[1] if isinstance(x, tuple) else x
        return self.S.op(eng, fn, r, w, dma)

    def mm(self, out, lhsT, rhs, start, stop):
        return self.op(PE, lambda e: e.matmul(out[0], lhsT[0], rhs[0], start=start, stop=stop),
                       [lhsT, rhs], [out])

    def act(self, out, in_, func, bias=0.0, scale=1.0):
        rd = [in_]
        b = bias
        s = scale
        if isinstance(bias, tuple):
            rd.append(bias)
            b = bias[0]
        if isinstance(scale, tuple):
            rd.append(scale)
            s = scale[0]
        return self.op(ACT, lambda e: e.activation(out[0], in_[0], func, bias=b, scale=s), rd, [out])

    def tt(self, out, in0, in1, op, eng=DVE):
        return self.op(eng, lambda e: e.tensor_tensor(out[0], in0[0], in1[0], op), [in0, in1], [out])

    def ts(self, out, in0, s1, s2, op0, op1=None, eng=DVE):
        rd = [in0]
        a1, a2 = s1, s2
        if isinstance(s1, tuple):
            rd.append(s1)
            a1 = s1[0]
        if isinstance(s2, tuple):
            rd.append(s2)
            a2 = s2[0]
        if op1 is None:
            return self.op(eng, lambda e: e.tensor_scalar(out[0], in0[0], a1, None, op0), rd, [out])
        return self.op(eng, lambda e: e.tensor_scalar(out[0], in0[0], a1, a2, op0, op1), rd, [out])

    def stt(self, out, in0, sc, in1, op0, op1, eng=DVE):
        rd = [in0, in1]
        a = sc
        if isinstance(sc, tuple):
            rd.append(sc)
            a = sc[0]
        return self.op(eng, lambda e: e.scalar_tensor_tensor(out[0], in0[0], a, in1[0], op0, op1), rd, [out])

    def copy(self, out, in_, eng=DVE):
        return self.op(eng, lambda e: e.tensor_copy(out[0], in_[0]), [in_], [out])

    def memset(self, out, val, eng=DVE):
        return self.op(eng, lambda e: e.memset(out[0], val), [], [out])

    def dma(self, eng, out, in_, key):
        return self.op(eng, lambda e: e.dma_start(out=out[0], in_=in_[0]), [in_], [out], dma=key)

    def vcol(self, col, p0=0, p1=128):
        return self.VEC.r(col, col + 1, p0, p1)

    def load_w(self, src_ap, nk, ncols, after=()):
        i = self.slot_i % NSLOT
        self.slot_i += 1
        sl = self.WS[i]
        dst = (sl.v[:, 0:nk, 0:ncols], sl.tk())
        self.dma(POOL, dst, (src_ap, list(after)), ("w", i))
        return sl

    def build(self):
        nc = self.nc
        with ExitStack() as es:
            nwords = 52992
            sb = es.enter_context(nc.sbuf_tensor("sb", [128, nwords], F32))
            self.PS = es.enter_context(nc.psum_tensor("ps", [128, 8, TT], F32))
            self.mem = Mem(sb, nwords)
            self.alloc_static()
            self.prologue()
            for s in range(self.nseg):
                self.segment(s)
            self.emit(es)
        return nc

    def alloc_static(self):
        m = self.mem
        self.H = m.alloc(F32, (8, SEG), True)
        self.XN = m.alloc(BF16, (8, SEG), True)
        self.Y = m.alloc(BF16, (8, SEG), True)
        self.WS = [m.alloc(BF16, (8, 512), True) for _ in range(NSLOT)]
        self.VEC = m.alloc(F32, (NV,), True)
        self.BS = m.alloc(F32, (4, 128), True)
        self.CST = m.alloc(F32, (3, 128), True)
        self.SEL = m.alloc(F32, (8, 128), True)
        self.WBD = m.alloc(BF16, (8, 128), True)
        self.WST = m.alloc(BF16, (8, 128), True)
        self.WFB = m.alloc(BF16, (8, 8), True)
        self.ONES = m.alloc(F32, (128,), True)
        self.ONESB = m.alloc(BF16, (128,))
        self.MSKB = m.alloc(BF16, (2, 128), True)
        self.CL = m.alloc(F32, (4,))
        self.CL2 = m.alloc(F32, (4,))
        self.NBF = m.alloc(F32, (1,))
        self.LST = m.alloc(F32, (4,), True)
        self.LHALO = m.alloc(F32, (4, 3))
        self.SHALO = m.alloc(F32, (4, 2))
        self.CCAR = m.alloc(F32, (1,))
        self.FHALO = m.alloc(F32, (2, 44, 2), True)
        self.NCT = m.alloc(F32, (32, 8), True)
        m.ptr = (m.ptr + 255) // 256 * 256
        self.work0 = m.ptr

    def wreset(self):
        self.mem.ptr = self.work0

    def prologue(self):
        m = self.mem
        self.wreset()
        self.S.burst_begin()
        self.dma(SP, self.VEC.r(), (self.d_vecs, []), "c0")
        self.dma(SP, self.BS.r(), (self.d_bs.rearrange("p a b -> p (a b)"), []), "c0")
        self.dma(SP, self.CST.r(), (self.d_cst.rearrange("p a b -> p (a b)"), []), "c0")
        self.dma(SP, self.SEL.r(0, None, 0, 8), (self.d_sel.rearrange("p a b -> p (a b)"), []), "c0")
        self.dma(POOL, self.WBD.r(), (self.d_wbd.rearrange("p a b -> p (a b)"), []), "c1")
        self.dma(POOL, self.WFB.r(), (self.d_wf.rearrange("p a b -> p (a b)"), []), "c1")
        wtmp = m.alloc(F32, (8, 128), True)
        self.dma(SP, wtmp.r(), (self.d_wst.rearrange("p a b -> p (a b)"), []), "c0")
        self.S.burst_end()
        tri = self.CST.v[:, 1, :].unsqueeze(1).to_broadcast([128, 8, 128])
        self.op(DVE, lambda e: e.tensor_tensor(self.WST.v, wtmp.v, tri, ALU.mult),
                [wtmp.r(), self.CST.r()], [self.WST.r()])
        self.memset(self.ONES.r(), 1.0)
        self.memset(self.ONESB.r(), 1.0)
        self.copy(self.MSKB.c(0), self.CST.c(0))
        self.copy(self.MSKB.c(1), self.CST.c(2))
        for b in (self.LST, self.LHALO, self.SHALO, self.CCAR, self.FHALO):
            self.memset(b.r(), 0.0)
        e_ = m.alloc(F32, (4,))
        y_ = m.alloc(F32, (4,))
        l_ = m.alloc(F32, (4,))
        lam = self.VEC.r(C_LAM, C_LAM + 4)
        self.act(e_.r(), lam, AF.Exp, scale=-1.0)
        self.ts(y_.r(), e_.r(), 1.0, None, ALU.add)
        self.act(l_.r(), y_.r(), AF.Ln)
        self.ts(y_.r(), y_.r(), -1.0, 1e-30, ALU.add, ALU.max)
        self.op(DVE, lambda e: e.reciprocal(y_.flat, y_.flat), [y_.r()], [y_.r()])
        self.tt(l_.r(), l_.r(), e_.r(), ALU.mult)
        self.tt(l_.r(), l_.r(), y_.r(), ALU.mult)
        self.ts(self.CL.r(), l_.r(), -8.0, None, ALU.mult)
        self.ts(self.CL2.r(), l_.r(), -16.0, None, ALU.mult)
        self.ts(self.NBF.r(0, 1, 0, 8), self.vcol(C_BF, 0, 8), -1.0, None, ALU.mult)

    def hload(self, s, tiles=None):
        for t in (range(NT) if tiles is None else tiles):
            self.S.burst_begin()
            for c in range(KC):
                src = self.d_x[c * 128:(c + 1) * 128, s * SEG + t * TT:s * SEG + (t + 1) * TT]
                self.dma(SP, self.H.c(c, t * TT, (t + 1) * TT), (src, []), ("hload", t))
            self.S.burst_end()

    def segment(self, s):
        if s == 0:
            self.hload(0)
        self.rmsnorm(C_N0, self.XN)
        self.even_mixer(s)
        self.rmsnorm(C_NF0, self.XN)
        self.ffn(0, s)
        if self.nlayers > 1:
            self.rmsnorm(C_N1, self.XN)
            self.odd_mixer(s)
            self.rmsnorm(C_NF1, self.XN)
            self.ffn(1, s)
        self.final(s)

    def rmsnorm(self, gcol, dst, f32out=None, reset=True):
        m = self.mem
        if reset:
            self.wreset()
        sq = [m.alloc(BF16, (TT,), True) for _ in range(4)]
        rs = [m.alloc(F32, (TT,), True) for _ in range(2)]
        k = 0
        for t in range(NT):
            b = self.bank()
            lo, hi = t * TT, (t + 1) * TT
            for c in range(KC):
                q = sq[k % 4]
                k += 1
                self.act(q.r(), self.H.c(c, lo, hi), AF.Square)
                self.mm(self.pv(b), self.ONESB.r(), q.r(), c == 0, c == KC - 1)
            r = rs[t % 2]
            self.act(r.r(), self.pv(b), AF.Sqrt, bias=EPS, scale=1.0 / D)
            self.op(DVE, lambda e, r=r: e.reciprocal(r.flat, r.flat), [r.r()], [r.r()])
            for c in range(KC):
                o = (f32out if f32out is not None else dst).c(c, lo, hi)
                self.stt(o, self.H.c(c, lo, hi), self.vcol(gcol + c), r.r(), ALU.mult, ALU.mult)

    def proj_chunk(self, slot, mcol, src, nk=KC):
        banks = []
        for t in range(NT):
            b = self.bank()
            banks.append(b)
            for k in range(nk):
                lhsT = (slot.v[:, k, mcol * 128:(mcol + 1) * 128], slot.tk())
                self.mm(self.pv(b), lhsT, src.c(k, t * TT, (t + 1) * TT), k == 0, k == nk - 1)
        return banks

    def proj_tile(self, slot, mcol, src, t, nk=KC):
        b = self.bank()
        for k in range(nk):
            lhsT = (slot.v[:, k, mcol * 128:(mcol + 1) * 128], slot.tk())
            self.mm(self.pv(b), lhsT, src.c(k, t * TT, (t + 1) * TT), k == 0, k == nk - 1)
        return b

    def out_proj(self, dw):
        wv = dw.rearrange("(k p) n -> p k n", p=128)
        sls = [self.load_w(wv[:, :, mt * 512:(mt + 1) * 512], KC, 512) for mt in range(2)]
        for t in range(NT):
            for mt in range(2):
                for mm_ in range(4):
                    b = self.proj_tile(sls[mt], mm_, self.Y, t)
                    h = self.H.c(mt * 4 + mm_, t * TT, (t + 1) * TT)
                    self.tt(h, self.pv(b), h, ALU.add)

    def even_mixer(self, s):
        m = self.mem
        self.wreset()
        wv = self.d_win0.rearrange("(k p) n -> p k n", p=128)
        names = ("XA", "XC", "XCB", "R", "I", "M", "HS", "G")
        sets = []
        for _ in range(2):
            d = {}
            for nm in names:
                if nm == "XA":
                    d[nm] = m.alloc(F32, (SEG + 3,), True)
                elif nm == "XCB":
                    d[nm] = m.alloc(BF16, (SEG,), True)
                else:
                    d[nm] = m.alloc(F32, (SEG,), True)
            sets.append(d)
        csets = [dict(CS=m.alloc(F32, (SEG,), True), CV=m.alloc(F32, (SEG + 2,), True),
                      SC=m.alloc(F32, (SEG,), True)) for _ in range(2)]
        s_xa = self.load_w(wv[:, :, 0:512], KC, 512)
        s_ga = self.load_w(wv[:, :, 512:1024], KC, 512)
        hl = self.H.c(KC - 1, TT, 2 * TT)[1]
        s_c = self.load_w(wv[:, :, 1024:1536], KC, 512, after=hl)
        s_b = self.load_w(wv[:, :, 1536:2048], KC, 512, after=hl)
        for jp in range(2):
            js = (2 * jp, 2 * jp + 1)
            B = {j: sets[j % 2] for j in js}
            bx = {j: [] for j in js}
            for t in range(NT):
                for j in js:
                    bx[j].append(self.proj_tile(s_xa, j, self.XN, t))
            for j in js:
                XA = B[j]["XA"]
                for t in range(NT):
                    self.act(XA.r(3 + t * TT, 3 + (t + 1) * TT), self.pv(bx[j][t]), AF.Identity)
                self.copy(XA.r(0, 3), self.LHALO.c(j))
            bg = {j: self.proj_chunk(s_ga, j, self.XN) for j in js}
            for j in js:
                self_G = B[j]["G"]
                for t in range(NT):
                    self.act(self_G.r(t * TT, (t + 1) * TT), self.pv(bg[j][t]), AF.Gelu_apprx_tanh)
            for j in js:
                XA, XC = B[j]["XA"], B[j]["XC"]
                for k in range(4):
                    w = self.vcol(C_LCW + k * 4 + j)
                    if k == 0:
                        self.ts(XC.r(), XA.r(0, SEG), w, self.vcol(C_LCB + j), ALU.mult, ALU.add)
                    else:
                        self.stt(XC.r(), XA.r(k, k + SEG), w, XC.r(), ALU.mult, ALU.add)
                self.copy(self.LHALO.c(j), XA.r(SEG, SEG + 3))
                self.act(B[j]["XCB"].r(), XC.r(), AF.Identity)
            br, bi = {}, {}
            for j in js:
                XCB = B[j]["XCB"]
                br[j], bi[j] = [], []
                for t in range(NT):
                    b1 = self.bank()
                    self.mm(self.pv(b1), self.WBD.c(j), XCB.r(t * TT, (t + 1) * TT), True, True)
                    b2 = self.bank()
                    self.mm(self.pv(b2), self.WBD.c(4 + j), XCB.r(t * TT, (t + 1) * TT), True, True)
                    br[j].append(b1)
                    bi[j].append(b2)
            for j in js:
                R, I = B[j]["R"], B[j]["I"]
                for t in range(NT):
                    self.act(R.r(t * TT, (t + 1) * TT), self.pv(br[j][t]), AF.Sigmoid, bias=self.vcol(C_BA + j))
                    self.act(I.r(t * TT, (t + 1) * TT), self.pv(bi[j][t]), AF.Sigmoid, bias=self.vcol(C_BX + j))
            for j in js:
                R, M = B[j]["R"], B[j]["M"]
                self.act(M.r(), R.r(), AF.Exp, scale=self.CL2.r(j, j + 1))
                self.act(R.r(), R.r(), AF.Exp, scale=self.CL.r(j, j + 1))
            for j in js:
                M = B[j]["M"]
                self.act(M.r(), M.r(), AF.Sqrt, bias=1.0, scale=-1.0)
            for j in js:
                R, I, M, HS, G, XC = (B[j][k] for k in ("R", "I", "M", "HS", "G", "XC"))
                self.tt(M.r(), M.r(), I.r(), ALU.mult)
                self.tt(M.r(), M.r(), XC.r(), ALU.mult)
                init = self.LST.flat[:, j:j + 1]
                self.op(DVE, lambda e, init=init, HS=HS, R=R, M=M: e.tensor_tensor_scan(
                    HS.flat, R.flat, M.flat, init, ALU.mult, ALU.add), [R.r(), M.r(), self.LST.r()], [HS.r()])
                self.copy(self.LST.r(j, j + 1), HS.r(SEG - 1, SEG))
                self.tt(self.Y.c(j), HS.r(), G.r(), ALU.mult)
        s_v = self.load_w(wv[:, :, 2048:2560], KC, 512)
        for j in range(4):
            CS, CV, SC = (csets[j % 2][k] for k in ("CS", "CV", "SC"))
            bc = self.proj_chunk(s_c, j, self.XN)
            bv = self.proj_chunk(s_v, j, self.XN)
            for t in range(NT):
                self.act(CS.r(t * TT, (t + 1) * TT), self.pv(bc[t]), AF.Identity)
                self.tt(CV.r(2 + t * TT, 2 + (t + 1) * TT), CS.r(t * TT, (t + 1) * TT), self.pv(bv[t]), ALU.mult)
            self.copy(CV.r(0, 2), self.SHALO.c(j))
            for k in range(3):
                w = self.vcol(C_SCW + k * 4 + j)
                if k == 0:
                    self.ts(SC.r(), CV.r(0, SEG), w, self.vcol(C_SCB + j), ALU.mult, ALU.add)
                else:
                    self.stt(SC.r(), CV.r(k, k + SEG), w, SC.r(), ALU.mult, ALU.add)
            self.copy(self.SHALO.c(j), CV.r(SEG, SEG + 2))
            bb = self.proj_chunk(s_b, j, self.XN)
            for t in range(NT):
                self.tt(self.Y.c(4 + j, t * TT, (t + 1) * TT), SC.r(t * TT, (t + 1) * TT), self.pv(bb[t]), ALU.mult)
        self.out_proj(self.d_wout0)

    def ffn(self, l, s):
        m = self.mem
        self.wreset()
        wu = self.d_up[l].rearrange("(k p) n -> p k n", p=128)
        ACTBS = [m.alloc(BF16, (8, SEG), True) for _ in range(2)]
        RAW = [m.alloc(F32, (SEG + 2,), True) for _ in range(4)]
        ACC = [m.alloc(F32, (SEG,), True) for _ in range(4)]
        SG = [m.alloc(F32, (SEG,), True) for _ in range(2)]
        cnt = [0]

        def up(tg):
            kg = tg // 2
            ACTB = ACTBS[kg % 2]
            ncol = 512 if tg < 5 else 256
            sg_ = self.load_w(wu[:, :, tg * 512:tg * 512 + ncol], KC, ncol)
            sv_ = self.load_w(wu[:, :, DFF + tg * 512:DFF + tg * 512 + ncol], KC, ncol)
            pre = {}
            if tg == 0:
                order = [(jj, w) for jj in range(2) for w in range(2)]
                for t in range(NT):
                    for jj, w in order:
                        pre.setdefault((jj, w), []).append(self.proj_tile((sg_, sv_)[w], jj, self.XN, t))
            for jj in range(ncol // 128):
                j = tg * 4 + jj
                jl = j - kg * 8
                accs = []
                for w, (slot, ch) in enumerate(((sg_, j), (sv_, NJ + j))):
                    raw = RAW[cnt[0] % 4]
                    acc = ACC[cnt[0] % 4]
                    cnt[0] += 1
                    banks = pre[(jj, w)] if (jj, w) in pre else self.proj_chunk(slot, jj, self.XN)
                    w0 = self.vcol(C_FCW + l * 132 + 0 * 44 + ch)
                    w1 = self.vcol(C_FCW + l * 132 + 1 * 44 + ch)
                    w2 = self.vcol(C_FCW + l * 132 + 2 * 44 + ch)
                    bb = self.vcol(C_FCB + l * 44 + ch)
                    for t in range(NT):
                        self.act(raw.r(2 + t * TT, 2 + (t + 1) * TT), self.pv(banks[t]), AF.Identity)
                        self.act(acc.r(t * TT, (t + 1) * TT), self.pv(banks[t]), AF.Identity, bias=bb, scale=w2)
                    hal = (self.FHALO.v[:, l, ch, :], self.FHALO.tk())
                    self.copy(raw.r(0, 2), hal)
                    self.stt(acc.r(), raw.r(0, SEG), w0, acc.r(), ALU.mult, ALU.add)
                    self.stt(acc.r(), raw.r(1, SEG + 1), w1, acc.r(), ALU.mult, ALU.add)
                    self.copy(hal, raw.r(SEG, SEG + 2))
                    accs.append(acc)
                sg = SG[j % 2]
                self.act(sg.r(), accs[0].r(), AF.Silu)
                self.tt(ACTB.c(jl), sg.r(), accs[1].r(), ALU.mult)

        def down(kg):
            ACTB = ACTBS[kg % 2]
            njg = 8 if kg < 2 else 6
            if kg == 2:
                sls = []
                for mh in range(2):
                    src = self.d_down[l][kg * 1024:kg * 1024 + njg * 128, mh * 512:(mh + 1) * 512]
                    sls.append(self.load_w(src.rearrange("(k p) n -> p k n", p=128), njg, 512))
                for t in range(NT):
                    for mh in range(2):
                        for mm_ in range(4):
                            b = self.proj_tile(sls[mh], mm_, ACTB, t, nk=njg)
                            h = self.H.c(mh * 4 + mm_, t * TT, (t + 1) * TT)
                            self.tt(h, self.pv(b), h, ALU.add)
                return
            for mh in range(2):
                src = self.d_down[l][kg * 1024:kg * 1024 + njg * 128, mh * 512:(mh + 1) * 512]
                sl = self.load_w(src.rearrange("(k p) n -> p k n", p=128), njg, 512)
                for mm_ in range(4):
                    banks = self.proj_chunk(sl, mm_, ACTB, nk=njg)
                    for t in range(NT):
                        h = self.H.c(mh * 4 + mm_, t * TT, (t + 1) * TT)
                        self.tt(h, self.pv(banks[t]), h, ALU.add)

        up(0)
        up(1)
        up(2)
        down(0)
        up(3)
        up(4)
        down(1)
        up(5)
        down(2)

    def odd_mixer(self, s):
        m = self.mem
        self.wreset()
        wv = self.d_win1.rearrange("(k p) n -> p k n", p=128)
        nblk = SEG // 128
        gb0 = s * nblk
        NB = (s + 1) * nblk
        SPb = m.alloc(F32, (SEG,), True)
        CSEG = m.alloc(F32, (SEG,), True)
        ON8 = m.alloc(F32, (SEG,), True)
        SQ = [m.alloc(F32, (TT,), True) for _ in range(2)]
        SS = m.alloc(F32, (nblk, 8), True)
        GVB = m.alloc(BF16, (nblk, 512), True)
        U = m.alloc(F32, (SEG,), True)
        MX = [m.alloc(F32, (TT,), True) for _ in range(2)]
        KS = m.alloc(BF16, (SEG,), True)
        mark = m.ptr
        KA = m.alloc(BF16, (2, 4 * SEG), True)
        end = m.ptr
        m.ptr = mark
        GE = [m.alloc(F32, (TT,), True) for _ in range(nblk)]
        assert m.ptr <= end
        m.ptr = mark
        VS = m.alloc(BF16, (nblk, 4 * 192), True)
        assert m.ptr <= end
        m.ptr = end
        QA = m.alloc(BF16, (2, SEG), True)
        CHI = m.alloc(BF16, (SEG,), True)
        VA = m.alloc(BF16, (32, 192), True)
        PTb = [m.alloc(BF16, (TT,), True) for _ in range(4)]
        RD = m.alloc(F32, (TT,), True)
        top = m.ptr
        m.ptr = self.work0
        KA2 = m.alloc(BF16, (2, 4 * SEG), True)
        VA2 = m.alloc(BF16, (32, 192), True)
        QA2 = m.alloc(BF16, (2, SEG), True)
        assert m.ptr <= KS.off, (m.ptr, KS.off)
        m.ptr = top
        KAs, VAs, QAs = (KA, KA2), (VA, VA2), (QA, QA2)
        P8 = dict(p0=0, p1=8)

        for t in range(NT):
            b = self.bank()
            for k in range(KC):
                self.mm(self.pv(b, 0, TT, 0, 8), (self.WFB.v[:, k, :], self.WFB.tk()),
                        self.XN.c(k, t * TT, (t + 1) * TT), k == 0, k == KC - 1)
            self.act(SPb.r(t * TT, (t + 1) * TT, **P8), self.pv(b, 0, TT, 0, 8), AF.Exp,
                     bias=self.NBF.r(0, 1, **P8), scale=-1.0)
        self.act(SPb.r(0, SEG, **P8), SPb.r(0, SEG, **P8), AF.Ln, bias=1.0)
        self.memset(ON8.r(0, SEG, **P8), 1.0)
        init = self.CCAR.flat[0:8, 0:1]
        self.op(DVE, lambda e: e.tensor_tensor_scan(CSEG.flat[0:8, :], ON8.flat[0:8, :], SPb.flat[0:8, :], init,
                                                    ALU.mult, ALU.subtract),
                [ON8.r(), SPb.r(), self.CCAR.r()], [CSEG.r()])
        self.copy(self.CCAR.r(0, 1, **P8), CSEG.r(SEG - 1, SEG, **P8))
        s_g = self.load_w(wv[:, :, 512:1024], KC, 512)
        s_u = self.load_w(wv[:, :, 0:512], KC, 512)
        for blk in range(nblk):
            b = self.bank()
            for k in range(KC):
                self.mm(self.pv(b), self.XN.c(k, blk * 128, (blk + 1) * 128), (s_g.v[:, k, :], s_g.tk()),
                        k == 0, k == KC - 1)
            ge, sq = GE[blk], SQ[blk % 2]
            self.act(ge.r(), self.pv(b), AF.Gelu_apprx_tanh)
            self.tt(sq.r(), ge.r(), ge.r(), ALU.mult)
            sq3 = sq.flat.rearrange("p (g c) -> p g c", g=8)
            ssb = SS.c(blk)
            self.op(DVE, lambda e, ssb=ssb, sq3=sq3: e.tensor_reduce(ssb[0], sq3, AX.X, ALU.add), [sq.r()], [ssb])
        self.act(SS.r(), SS.r(), AF.Sqrt, bias=EPS, scale=1.0 / 64)
        self.op(DVE, lambda e: e.reciprocal(SS.flat, SS.flat), [SS.r()], [SS.r()])
        for blk in range(nblk):
            ge = GE[blk]
            ge3 = ge.flat.rearrange("p (g c) -> p g c", g=8)
            gv3 = GVB.v[:, blk, :].rearrange("p (g c) -> p g c", g=8)
            bc = SS.v[:, blk, :].unsqueeze(2).to_broadcast([128, 8, 64])
            self.op(DVE, lambda e, gv3=gv3, ge3=ge3, bc=bc: e.tensor_tensor(gv3, ge3, bc, ALU.mult),
                    [ge.r(), SS.r()], [GVB.c(blk)])

        for cc in range(4):
            bu = self.proj_chunk(s_u, cc, self.XN)
            for t in range(NT):
                self.act(U.r(t * TT, (t + 1) * TT), self.pv(bu[t]), AF.Gelu_apprx_tanh)
            for t in range(NT):
                b = self.bank()
                for bl in range(4):
                    blk = t * 4 + bl
                    for hh in range(2):
                        h = 2 * cc + hh
                        out = (self.PS[hh * 64:(hh + 1) * 64, b, bl * 128:(bl + 1) * 128], [self.ps_tok[b]])
                        lhsT = (GVB.v[:, blk, h * 64:(h + 1) * 64], GVB.c(blk)[1])
                        rhs = (self.WST.v[:, h, :], self.WST.tk())
                        self.mm(out, lhsT, rhs, True, True)
                mx = MX[t % 2]
                mx3 = mx.flat.rearrange("p (a b) -> p a b", a=4)
                ps3 = self.PS[:, b, :].rearrange("p (a b) -> p a b", a=4)
                bs3 = self.BS.v[:, cc, :].unsqueeze(1).to_broadcast([128, 4, 128])
                gn = self.vcol(C_GN + cc)
                self.op(DVE, lambda e, mx3=mx3, ps3=ps3, bs3=bs3, gn=gn: e.scalar_tensor_tensor(
                    mx3, ps3, gn[0], bs3, ALU.mult, ALU.add), [self.pv(b), gn, self.BS.r()], [mx.r()])
                self.tt(self.Y.c(cc, t * TT, (t + 1) * TT), mx.r(), U.r(t * TT, (t + 1) * TT), ALU.mult)

        ident8 = (self.CST.v[0:8, 2, 0:8], self.CST.tk())
        for blk in range(nblk):
            b = self.bank()
            src = CSEG.r(blk * 128, (blk + 1) * 128, **P8)
            out = self.pv(b, 0, 8)
            self.op(PE, lambda e, out=out, src=src: e.transpose(out[0], src[0], ident8[0]), [src, ident8], [out])
            dst = (self.NCT.v[:, gb0 + blk, :], self.NCT.tk())
            self.ts(dst, out, -1.0, None, ALU.mult)

        s_q = self.load_w(wv[:, :, 1024:1536], KC, 512)
        s_k = self.load_w(wv[:, :, 1536:2048], KC, 512)
        s_v = self.load_w(wv[:, :, 2048:2560], KC, 512)
        vs5 = VS.flat.rearrange("p (b c x w) -> p b c x w", b=nblk, c=4, x=3)
        self.memset((vs5[:, :, :, 1, :], VS.tk()), 1.0)
        for blk in range(nblk):
            b = self.bank()
            for k in range(KC):
                self.mm(self.pv(b), self.XN.c(k, blk * 128, (blk + 1) * 128), (s_v.v[:, k, :], s_v.tk()),
                        k == 0, k == KC - 1)
            ps4 = self.PS[:, b, :].rearrange("p (c x w) -> p c x w", c=4, x=2)
            for hh in range(2):
                self.act((vs5[:, blk, :, 2 * hh, :], VS.c(blk)[1]), (ps4[:, :, hh, :], [self.ps_tok[b]]), AF.Identity)
        self.S.burst_begin()
        for cc in range(4):
            vdst = self.d_va[cc][:, gb0:gb0 + nblk, :]
            self.dma(SP, (vdst, [self.tok_va]), (VS.v[:, :, cc * 192:(cc + 1) * 192], VS.tk()), "vst")
        self.S.burst_end()

        self.act(CHI.r(0, SEG, **P8), CSEG.r(0, SEG, **P8), AF.Identity)
        SK = 3
        it = 0

        def prep(cc):
            QA, KA, VA = QAs[cc % 2], KAs[cc % 2], VAs[cc % 2]
            bq = self.proj_chunk(s_q, cc, self.XN)
            for t in range(NT):
                self.act((QA.v[0:64, 0, t * TT:(t + 1) * TT], QA.tk()), self.pv(bq[t], 0, TT, 0, 64),
                         AF.Identity, scale=0.125)
                self.ts((QA.v[0:64, 1, t * TT:(t + 1) * TT], QA.tk()), self.pv(bq[t], 0, TT, 64, 128),
                        0.125, None, ALU.mult)
            bk = self.proj_chunk(s_k, cc, self.XN)
            for t in range(NT):
                self.act(KS.r(t * TT, (t + 1) * TT), self.pv(bk[t]), AF.Identity)
            self.dma(SP, (self.d_kt[cc][:, s * SEG:(s + 1) * SEG], [self.tok_kt[cc]]), KS.r(), "kst")
            key = ("att", cc % 2)
            self.S.burst_begin()
            for hh in range(2):
                h = 2 * cc + hh
                self.dma(SP, (KA.v[0:64, hh, 0:NB * 128], KA.tk()),
                         (self.d_kt[cc][hh * 64:(hh + 1) * 64, 0:NB * 128], [self.tok_kt[cc]]), key)
                self.dma(SP, (QA.v[64:65, hh, :], QA.tk()), (CHI.flat[h:h + 1, :], CHI.tk()), key)
            self.dma(SP, (VA.v[:, 0:NB, :], VA.tk()), (self.d_va[cc][:, 0:NB, :], [self.tok_va]), key)
            self.S.burst_end()
            self.memset((KA.v[64:65, :, 0:NB * 128], KA.tk()), 1.0)

        prep(0)
        for cc in range(4):
            if cc + 1 < 4:
                prep(cc + 1)
            QA, KA, VA = QAs[cc % 2], KAs[cc % 2], VAs[cc % 2]
            steps = []
            for hh in range(2):
                for t in range(NT):
                    Q0 = gb0 + 4 * t
                    nJ = Q0 + 4
                    ob = self.bank(6, 8)
                    for J in range(nJ):
                        steps.append((hh, t, J, Q0, nJ, ob))
            pend = {}
            n = len(steps)
            for i in range(n + SK):
                if i < n:
                    hh, t, J, Q0, nJ, ob = steps[i]
                    h = 2 * cc + hh
                    c0 = max(0, J - Q0) * 128
                    sb_ = self.bank(0, 6)
                    pt = PTb[it % 4]
                    it += 1
                    self.mm(self.pv(sb_, c0, TT), (KA.v[0:65, hh, J * 128:(J + 1) * 128], KA.tk()),
                            (QA.v[0:65, hh, t * TT + c0:(t + 1) * TT], QA.tk()), True, True)
                    bias = (self.NCT.v[:, J, h:h + 1], self.NCT.tk())
                    if J >= Q0:
                        self.mm(self.pv(sb_, c0, c0 + 128), self.MSKB.c(1), self.MSKB.c(0), False, True)
                    self.act(pt.r(c0, TT), self.pv(sb_, c0, TT), AF.Exp, bias=bias)
                    pend[i] = (pt, c0)
                j = i - SK
                if j >= 0:
                    hh, t, J, Q0, nJ, ob = steps[j]
                    pt, c0 = pend.pop(j)
                    self.mm(self.pv(ob, c0, TT), (VA.v[:, J, hh * 64:hh * 64 + 128], VA.tk()), pt.r(c0, TT),
                            J == 0, J == nJ - 1)
                    if J == nJ - 1:
                        p0, p1 = hh * 64, (hh + 1) * 64
                        d0, d1 = (1 - hh) * 64, (2 - hh) * 64
                        den = self.pv(ob, 0, TT, d0, d1)
                        num = self.pv(ob, 0, TT, p0, p1)
                        rd = RD.r(0, TT, p0, p1)
                        self.op(DVE, lambda e, rd=rd, den=den: e.reciprocal(rd[0], den[0]), [den], [rd])
                        self.tt(self.Y.c(4 + cc, t * TT, (t + 1) * TT, p0, p1), num, rd, ALU.mult)
        self.out_proj(self.d_wout1)

    def final(self, s):
        m = self.mem
        self.wreset()
        XO = m.alloc(F32, (8, SEG), True)
        self.rmsnorm(C_NFIN, None, f32out=XO, reset=False)
        for t in range(NT):
            if s + 1 < self.nseg:
                self.hload(s + 1, tiles=[t])
            self.S.burst_begin()
            for c in range(KC):
                dst = self.d_y[c * 128:(c + 1) * 128, s * SEG + t * TT:s * SEG + (t + 1) * TT]
                ev = self.dma(SP, (dst, []), XO.c(c, t * TT, (t + 1) * TT), ("ystore", t))
                self.out_evs.append(ev)
            self.S.burst_end()

    def emit(self, es):
        nc = self.nc
        S = self.S
        csem = {e: es.enter_context(nc.semaphore("s_" + e)) for e in (PE, ACT, DVE, POOL)}
        dsem = {k: es.enter_context(nc.semaphore("d%d" % i)) for i, k in enumerate(S.dma_cnt.keys())}
        for e in ENGS:
            c = 0
            for fn, deps, ev in S.ops[e]:
                if not ev.is_dma and ev.needed:
                    c += 1
                    ev.val = c
            print("engine", e, "ops", len(S.ops[e]), "incs", c, flush=True)
        out_evs = self.out_evs

        def run(name, eng, final=False):
            waited = {}

            def wait(d):
                key = ("d", d.semkey) if d.is_dma else ("c", d.eng)
                if waited.get(key, 0) >= d.val:
                    return
                sem = dsem[d.semkey] if d.is_dma else csem[d.eng]
                eng.wait_ge(sem, d.val)
                waited[key] = d.val

            for fn, deps, ev in S.ops[name]:
                for d in deps:
                    wait(d)
                ins = fn(eng)
                if ev.is_dma:
                    ins.then_inc(dsem[ev.semkey], 16)
                elif ev.needed:
                    ins.then_inc(csem[name], 1)
            if final:
                for d in out_evs:
                    wait(d)

        block = es.enter_context(nc.Block())

        @block.tensor
        def _(e):
            run(PE, e)

        @block.scalar
        def _(e):
            run(ACT, e)

        @block.vector
        def _(e):
            run(DVE, e)

        @block.gpsimd
        def _(e):
            run(POOL, e)

        @block.sync
        def _(e):
            run(SP, e, final=True)


def pack_vecs(inp):
    v = np.zeros((128, NV), np.float32)

    def fm(vec):
        return np.ascontiguousarray(np.asarray(vec, np.float32).reshape(-1, 128).T)

    v[:, C_N0:C_N0 + 8] = fm(inp["mix0_norm"][0])
    v[:, C_N1:C_N1 + 8] = fm(inp["mix1_norm"][0])
    v[:, C_NF0:C_NF0 + 8] = fm(inp["ffn_norm"][0])
    v[:, C_NF1:C_NF1 + 8] = fm(inp["ffn_norm"][1])
    v[:, C_NFIN:C_NFIN + 8] = fm(inp["final_norm"])
    for k in range(4):
        v[:, C_LCW + k * 4:C_LCW + k * 4 + 4] = fm(inp["lru_conv_w"][0, k])
    v[:, C_LCB:C_LCB + 4] = fm(inp["lru_conv_b"][0])
    v[:, C_BA:C_BA + 4] = fm(inp["lru_ba"][0])
    v[:, C_BX:C_BX + 4] = fm(inp["lru_bx"][0])
    v[:, C_LAM:C_LAM + 4] = fm(inp["lru_lambda"][0])
    for k in range(3):
        v[:, C_SCW + k * 4:C_SCW + k * 4 + 4] = fm(inp["sconv_w"][0, k])
    v[:, C_SCB:C_SCB + 4] = fm(inp["sconv_b"][0])
    v[:, C_GN:C_GN + 4] = fm(inp["sgu_norm"][0])
    for l in range(2):
        for k in range(3):
            c0 = C_FCW + l * 132 + k * 44
            v[:, c0:c0 + 44] = fm(inp["ffn_conv_w"][l, k])
        c0 = C_FCB + l * 44
        v[:, c0:c0 + 44] = fm(inp["ffn_conv_b"][l])
    v[0:8, C_BF] = np.asarray(inp["fox_bf"][0], np.float32)
    return v


def host_consts(inp):
    wa = np.asarray(inp["lru_wa"][0], np.float32)
    wx = np.asarray(inp["lru_wx"][0], np.float32)
    wbd = np.zeros((128, 8, 128), np.float32)
    for j in range(4):
        for hh in range(2):
            wbd[hh * 64:(hh + 1) * 64, j, hh * 64:(hh + 1) * 64] = wa[2 * j + hh]
            wbd[hh * 64:(hh + 1) * 64, 4 + j, hh * 64:(hh + 1) * 64] = wx[2 * j + hh]
    sw = np.asarray(inp["sgu_w"][0], np.float32)
    wst = np.ascontiguousarray(sw.transpose(2, 0, 1))
    w1 = np.asarray(inp["mix1_w_in"][0], np.float32)
    wf = np.ascontiguousarray(w1[:, 2560:2568].reshape(8, 128, 8).transpose(1, 0, 2))
    sb = np.asarray(inp["sgu_b"][0], np.float32)
    bs = np.zeros((128, 4, 128), np.float32)
    for cc in range(4):
        bs[0:64, cc, :] = sb[2 * cc][None, :]
        bs[64:128, cc, :] = sb[2 * cc + 1][None, :]
    k = np.arange(128)[:, None]
    q = np.arange(128)[None, :]
    cst = np.zeros((128, 3, 128), np.float32)
    cst[:, 0, :] = np.where(k > q, NEG, 0.0)
    cst[:, 1, :] = (k <= q).astype(np.float32)
    cst[:, 2, :] = np.eye(128, dtype=np.float32)
    sel = np.zeros((8, 8, 128), np.float32)
    for h in range(8):
        sel[h, h, :] = 1.0
    return dict(wbd=wbd, wst=wst, wf=wf, bs=bs, cst=cst, sel=sel, vecs=pack_vecs(inp))


_CACHE = {}


def run_prog(inp, nseg=4, nlayers=2, batches=(0, 1, 2, 3), ncores=8):
    key = (nseg, nlayers)
    if key not in _CACHE:
        p = Prog(nseg=nseg, nlayers=nlayers)
        p.build()
        _CACHE[key] = p
    p = _CACHE[key]
    hc = host_consts(inp)
    x = np.asarray(inp["x"], np.float32)
    T = nseg * SEG
    shared = dict(
        w_in0=np.ascontiguousarray(np.asarray(inp["mix0_w_in"][0], np.float32)),
        w_out0=np.ascontiguousarray(np.asarray(inp["mix0_w_out"][0], np.float32)),
        w_in1=np.ascontiguousarray(np.asarray(inp["mix1_w_in"][0], np.float32)),
        w_out1=np.ascontiguousarray(np.asarray(inp["mix1_w_out"][0], np.float32)),
        w_up=np.ascontiguousarray(np.asarray(inp["ffn_up"], np.float32)),
        w_down=np.ascontiguousarray(np.asarray(inp["ffn_down"], np.float32)),
        **hc,
    )
    stride = 2 if ncores >= 2 * len(batches) else 1
    zeros = None
    maps = []
    for c in range(ncores):
        if c % stride == 0 and c // stride < len(batches):
            m = dict(shared)
            m["xT"] = np.ascontiguousarray(x[batches[c // stride], :T, :].T)
        else:
            if zeros is None:
                zeros = {k: np.zeros_like(v) for k, v in shared.items()}
                zeros["xT"] = np.zeros((D, T), np.float32)
            m = zeros
        maps.append(m)
    res = run_bass_kernel_spmd(p.nc, maps, core_ids=list(range(ncores)))
    outs = [np.ascontiguousarray(res.results[i * stride]["yT"].T) for i in range(len(batches))]
    return outs


def kernel(**inputs):
    outs = run_prog(inputs, nseg=4, nlayers=2)
    return np.stack(outs, axis=0).astype(np.float32)
```

```python
import math
from contextlib import ExitStack

import numpy as np
import concourse.bass as bass
import concourse.mybir as mybir
from concourse.bass_utils import run_bass_kernel_spmd

F32 = mybir.dt.float32
BF16 = mybir.dt.bfloat16
AF = mybir.ActivationFunctionType
ALU = mybir.AluOpType
AX = mybir.AxisListType

PE, ACT, DVE, POOL, SP = "pe", "act", "dve", "pool", "sp"
ENGS = [PE, ACT, DVE, POOL, SP]

SEG = 1024
TT = 512
NT = SEG // TT
D = 1024
KC = 8
DFF = 2816
NJ = 22
EPS = 1e-6
NV = 448
NSLOT = 4
NEG = -30000.0

C_N0, C_N1, C_NF0, C_NF1, C_NFIN = 0, 8, 16, 24, 32
C_LCW, C_LCB, C_BA, C_BX, C_LAM = 40, 56, 60, 64, 68
C_SCW, C_SCB, C_GN = 72, 84, 88
C_FCW, C_FCB, C_BF = 92, 356, 444


class Tok:
    __slots__ = ("w", "rs")

    def __init__(self):
        self.w = None
        self.rs = []


class Ev:
    __slots__ = ("eng", "val", "needed", "is_dma", "semkey")

    def __init__(self, eng):
        self.eng = eng
        self.val = 0
        self.needed = False
        self.is_dma = False
        self.semkey = None


class Sched:
    def __init__(self):
        self.ops = {e: [] for e in ENGS}
        self.dma_cnt = {}
        self.burst = None

    def burst_begin(self):
        self.burst = []

    def burst_end(self):
        last = {}
        for ev in self.burst:
            last[ev.semkey] = max(last.get(ev.semkey, 0), ev.val)
        for ev in self.burst:
            ev.val = last[ev.semkey]
        self.burst = None

    def op(self, eng, fn, reads=(), writes=(), dma=None):
        ev = Ev(eng)
        if dma is not None:
            ev.is_dma = True
            ev.semkey = dma
            self.dma_cnt[dma] = self.dma_cnt.get(dma, 0) + 16
            ev.val = self.dma_cnt[dma]
            if self.burst is not None:
                self.burst.append(ev)
        deps = {}

        def add(d):
            if d is None:
                return
            if eng == PE and d.eng == PE and not d.is_dma:
                return
            if self.burst is not None and d in self.burst:
                return
            deps[id(d)] = d

        for t in reads:
            add(t.w)
        for t in writes:
            add(t.w)
            for r in t.rs:
                add(r)
        for d in deps.values():
            d.needed = True
        for t in reads:
            if not ev.is_dma:
                t.rs = [r for r in t.rs if r.is_dma or r.eng != eng]
            t.rs.append(ev)
        for t in writes:
            t.w = ev
            t.rs = []
        self.ops[eng].append((fn, list(deps.values()), ev))
        return ev


class Mem:
    PAGE = 256

    def __init__(self, sb, nwords):
        self.sb = sb
        self.n = nwords
        self.toks = [Tok() for _ in range((nwords + self.PAGE - 1) // self.PAGE)]
        self.ptr = 0
        self.hi = 0

    def alloc(self, dtype, shape, page_align=False):
        esz = 4 if dtype == F32 else 2
        n = int(np.prod(shape))
        words = (n * esz + 3) // 4
        words = (words + 7) // 8 * 8
        if page_align:
            self.ptr = (self.ptr + self.PAGE - 1) // self.PAGE * self.PAGE
        off = self.ptr
        self.ptr += words
        self.hi = max(self.hi, self.ptr)
        assert self.ptr <= self.n, f"SBUF overflow {self.ptr} > {self.n}"
        return Buf(self, off, dtype, tuple(shape))


class Buf:
    def __init__(self, mem, off, dtype, shape):
        self.mem = mem
        self.off = off
        self.dtype = dtype
        self.shape = shape
        self.esz = 4 if dtype == F32 else 2
        self.n = int(np.prod(shape))
        words = (self.n * self.esz + 3) // 4
        base = mem.sb[:, off:off + words]
        flat = base if dtype == F32 else base.bitcast(dtype)
        self.flat = flat[:, 0:self.n]
        if len(shape) == 1:
            self.v = self.flat
        elif len(shape) == 2:
            self.v = self.flat.rearrange("p (a b) -> p a b", a=shape[0])
        else:
            self.v = self.flat.rearrange("p (a b c) -> p a b c", a=shape[0], b=shape[1])

    def tk(self, lo=0, hi=None):
        if hi is None:
            hi = self.n
        b0 = self.off * 4 + lo * self.esz
        b1 = self.off * 4 + hi * self.esz
        return self.mem.toks[b0 // 1024:(b1 - 1) // 1024 + 1]

    def r(self, lo=0, hi=None, p0=0, p1=128):
        if hi is None:
            hi = self.n
        return (self.flat[p0:p1, lo:hi], self.tk(lo, hi))

    def c(self, a, lo=0, hi=None, p0=0, p1=128):
        B = self.shape[-1] if len(self.shape) == 2 else self.shape[1] * self.shape[2]
        if hi is None:
            hi = B
        return (self.flat[p0:p1, a * B + lo:a * B + hi], self.tk(a * B + lo, a * B + hi))


class Prog:
    def __init__(self, nseg=4, nlayers=2, debug=False):
        self.nseg = nseg
        self.nlayers = nlayers
        self.S = Sched()
        self.nc = bass.Bass("TRN2", target_bir_lowering=False)
        nc = self.nc
        self.T = nseg * SEG
        T = self.T
        dt = nc.dram_tensor
        self.d_x = dt("xT", [D, T], F32, kind="ExternalInput").ap()
        self.d_win0 = dt("w_in0", [D, 2560], F32, kind="ExternalInput").ap()
        self.d_wout0 = dt("w_out0", [D, D], F32, kind="ExternalInput").ap()
        self.d_win1 = dt("w_in1", [D, 2568], F32, kind="ExternalInput").ap()
        self.d_wout1 = dt("w_out1", [D, D], F32, kind="ExternalInput").ap()
        self.d_up = dt("w_up", [2, D, 2 * DFF], F32, kind="ExternalInput").ap()
        self.d_down = dt("w_down", [2, DFF, D], F32, kind="ExternalInput").ap()
        self.d_wbd = dt("wbd", [128, 8, 128], F32, kind="ExternalInput").ap()
        self.d_wst = dt("wst", [128, 8, 128], F32, kind="ExternalInput").ap()
        self.d_wf = dt("wf", [128, 8, 8], F32, kind="ExternalInput").ap()
        self.d_vecs = dt("vecs", [128, NV], F32, kind="ExternalInput").ap()
        self.d_bs = dt("bs", [128, 4, 128], F32, kind="ExternalInput").ap()
        self.d_cst = dt("cst", [128, 3, 128], F32, kind="ExternalInput").ap()
        self.d_sel = dt("sel", [8, 8, 128], F32, kind="ExternalInput").ap()
        self.d_y = dt("yT", [D, T], F32, kind="ExternalOutput").ap()
        self.d_kt = dt("kt_scratch", [4, 128, T], BF16).ap()
        self.d_va = dt("va_scratch", [4, 128, T // 128, 192], BF16).ap()
        self.tok_kt = [Tok() for _ in range(4)]
        self.tok_va = Tok()
        self.out_evs = []
        self.ps_tok = [Tok() for _ in range(8)]
        self.ps_i = 0
        self.slot_i = 0
        self.rot = {}

    def bank(self, lo=0, hi=8):
        key = (lo, hi)
        i = self.rot.get(key, 0)
        self.rot[key] = i + 1
        b = lo + i % (hi - lo)
        return b

    def pv(self, b, lo=0, hi=TT, p0=0, p1=128):
        return (self.PS[p0:p1, b, lo:hi], [self.ps_tok[b]])

    def op(self, eng, fn, reads, writes, dma=None):
        r = []
        for x in reads:
            r += x[1] if isinstance(x, tuple) else x
        w = []
        for x in writes:
            w += x[1] if isinstance(x, tuple) else x
        return self.S.op(eng, fn, r, w, dma)

    def mm(self, out, lhsT, rhs, start, stop):
        return self.op(PE, lambda e: e.matmul(out[0], lhsT[0], rhs[0], start=start, stop=stop),
                       [lhsT, rhs], [out])

    def act(self, out, in_, func, bias=0.0, scale=1.0):
        rd = [in_]
        b = bias
        s = scale
        if isinstance(bias, tuple):
            rd.append(bias)
            b = bias[0]
        if isinstance(scale, tuple):
            rd.append(scale)
            s = scale[0]
        return self.op(ACT, lambda e: e.activation(out[0], in_[0], func, bias=b, scale=s), rd, [out])

    def tt(self, out, in0, in1, op, eng=DVE):
        return self.op(eng, lambda e: e.tensor_tensor(out[0], in0[0], in1[0], op), [in0, in1], [out])

    def ts(self, out, in0, s1, s2, op0, op1=None, eng=DVE):
        rd = [in0]
        a1, a2 = s1, s2
        if isinstance(s1, tuple):
            rd.append(s1)
            a1 = s1[0]
        if isinstance(s2, tuple):
            rd.append(s2)
            a2 = s2[0]
        if op1 is None:
            return self.op(eng, lambda e: e.tensor_scalar(out[0], in0[0], a1, None, op0), rd, [out])
        return self.op(eng, lambda e: e.tensor_scalar(out[0], in0[0], a1, a2, op0, op1), rd, [out])

    def stt(self, out, in0, sc, in1, op0, op1, eng=DVE):
        rd = [in0, in1]
        a = sc
        if isinstance(sc, tuple):
            rd.append(sc)
            a = sc[0]
        return self.op(eng, lambda e: e.scalar_tensor_tensor(out[0], in0[0], a, in1[0], op0, op1), rd, [out])

    def copy(self, out, in_, eng=DVE):
        return self.op(eng, lambda e: e.tensor_copy(out[0], in_[0]), [in_], [out])

    def memset(self, out, val, eng=DVE):
        return self.op(eng, lambda e: e.memset(out[0], val), [], [out])

    def dma(self, eng, out, in_, key):
        return self.op(eng, lambda e: e.dma_start(out=out[0], in_=in_[0]), [in_], [out], dma=key)

    def vcol(self, col, p0=0, p1=128):
        return self.VEC.r(col, col + 1, p0, p1)

    def load_w(self, src_ap, nk, ncols, after=()):
        i = self.slot_i % NSLOT
        self.slot_i += 1
        sl = self.WS[i]
        dst = (sl.v[:, 0:nk, 0:ncols], sl.tk())
        self.dma(POOL, dst, (src_ap, list(after)), ("w", i))
        return sl

    def build(self):
        nc = self.nc
        with ExitStack() as es:
            nwords = 52992
            sb = es.enter_context(nc.sbuf_tensor("sb", [128, nwords], F32))
            self.PS = es.enter_context(nc.psum_tensor("ps", [128, 8, TT], F32))
            self.mem = Mem(sb, nwords)
            self.alloc_static()
            self.prologue()
            for s in range(self.nseg):
                self.segment(s)
            self.emit(es)
        return nc

    def alloc_static(self):
        m = self.mem
        self.H = m.alloc(F32, (8, SEG), True)
        self.XN = m.alloc(BF16, (8, SEG), True)
        self.Y = m.alloc(BF16, (8, SEG), True)
        self.WS = [m.alloc(BF16, (8, 512), True) for _ in range(NSLOT)]
        self.VEC = m.alloc(F32, (NV,), True)
        self.BS = m.alloc(F32, (4, 128), True)
        self.CST = m.alloc(F32, (3, 128), True)
        self.SEL = m.alloc(F32, (8, 128), True)
        self.WBD = m.alloc(BF16, (8, 128), True)
        self.WST = m.alloc(BF16, (8, 128), True)
        self.WFB = m.alloc(BF16, (8, 8), True)
        self.ONES = m.alloc(F32, (128,), True)
        self.ONESB = m.alloc(BF16, (128,))
        self.MSKB = m.alloc(BF16, (2, 128), True)
        self.CL = m.alloc(F32, (4,))
        self.CL2 = m.alloc(F32, (4,))
        self.NBF = m.alloc(F32, (1,))
        self.LST = m.alloc(F32, (4,), True)
        self.LHALO = m.alloc(F32, (4, 3))
        self.SHALO = m.alloc(F32, (4, 2))
        self.CCAR = m.alloc(F32, (1,))
        self.FHALO = m.alloc(F32, (2, 44, 2), True)
        self.NCT = m.alloc(F32, (32, 8), True)
        m.ptr = (m.ptr + 255) // 256 * 256
        self.work0 = m.ptr

    def wreset(self):
        self.mem.ptr = self.work0

    def prologue(self):
        m = self.mem
        self.wreset()
        self.S.burst_begin()
        self.dma(SP, self.VEC.r(), (self.d_vecs, []), "c0")
        self.dma(SP, self.BS.r(), (self.d_bs.rearrange("p a b -> p (a b)"), []), "c0")
        self.dma(SP, self.CST.r(), (self.d_cst.rearrange("p a b -> p (a b)"), []), "c0")
        self.dma(SP, self.SEL.r(0, None, 0, 8), (self.d_sel.rearrange("p a b -> p (a b)"), []), "c0")
        self.dma(POOL, self.WBD.r(), (self.d_wbd.rearrange("p a b -> p (a b)"), []), "c1")
        self.dma(POOL, self.WFB.r(), (self.d_wf.rearrange("p a b -> p (a b)"), []), "c1")
        wtmp = m.alloc(F32, (8, 128), True)
        self.dma(SP, wtmp.r(), (self.d_wst.rearrange("p a b -> p (a b)"), []), "c0")
        self.S.burst_end()
        tri = self.CST.v[:, 1, :].unsqueeze(1).to_broadcast([128, 8, 128])
        self.op(DVE, lambda e: e.tensor_tensor(self.WST.v, wtmp.v, tri, ALU.mult),
                [wtmp.r(), self.CST.r()], [self.WST.r()])
        self.memset(self.ONES.r(), 1.0)
        self.memset(self.ONESB.r(), 1.0)
        self.copy(self.MSKB.c(0), self.CST.c(0))
        self.copy(self.MSKB.c(1), self.CST.c(2))
        for b in (self.LST, self.LHALO, self.SHALO, self.CCAR, self.FHALO):
            self.memset(b.r(), 0.0)
        e_ = m.alloc(F32, (4,))
        y_ = m.alloc(F32, (4,))
        l_ = m.alloc(F32, (4,))
        lam = self.VEC.r(C_LAM, C_LAM + 4)
        self.act(e_.r(), lam, AF.Exp, scale=-1.0)
        self.ts(y_.r(), e_.r(), 1.0, None, ALU.add)
        self.act(l_.r(), y_.r(), AF.Ln)
        self.ts(y_.r(), y_.r(), -1.0, 1e-30, ALU.add, ALU.max)
        self.op(DVE, lambda e: e.reciprocal(y_.flat, y_.flat), [y_.r()], [y_.r()])
        self.tt(l_.r(), l_.r(), e_.r(), ALU.mult)
        self.tt(l_.r(), l_.r(), y_.r(), ALU.mult)
        self.ts(self.CL.r(), l_.r(), -8.0, None, ALU.mult)
        self.ts(self.CL2.r(), l_.r(), -16.0, None, ALU.mult)
        self.ts(self.NBF.r(0, 1, 0, 8), self.vcol(C_BF, 0, 8), -1.0, None, ALU.mult)

    def hload(self, s, tiles=None):
        for t in (range(NT) if tiles is None else tiles):
            self.S.burst_begin()
            for c in range(KC):
                src = self.d_x[c * 128:(c + 1) * 128, s * SEG + t * TT:s * SEG + (t + 1) * TT]
                self.dma(SP, self.H.c(c, t * TT, (t + 1) * TT), (src, []), ("hload", t))
            self.S.burst_end()

    def segment(self, s):
        if s == 0:
            self.hload(0)
        self.rmsnorm(C_N0, self.XN)
        self.even_mixer(s)
        self.rmsnorm(C_NF0, self.XN)
        self.ffn(0, s)
        if self.nlayers > 1:
            self.rmsnorm(C_N1, self.XN)
            self.odd_mixer(s)
            self.rmsnorm(C_NF1, self.XN)
            self.ffn(1, s)
        self.final(s)

    def rmsnorm(self, gcol, dst, f32out=None, reset=True):
        m = self.mem
        if reset:
            self.wreset()
        sq = [m.alloc(BF16, (TT,), True) for _ in range(4)]
        rs = [m.alloc(F32, (TT,), True) for _ in range(2)]
        k = 0
        for t in range(NT):
            b = self.bank()
            lo, hi = t * TT, (t + 1) * TT
            for c in range(KC):
                q = sq[k % 4]
                k += 1
                self.act(q.r(), self.H.c(c, lo, hi), AF.Square)
                self.mm(self.pv(b), self.ONESB.r(), q.r(), c == 0, c == KC - 1)
            r = rs[t % 2]
            self.act(r.r(), self.pv(b), AF.Sqrt, bias=EPS, scale=1.0 / D)
            self.op(DVE, lambda e, r=r: e.reciprocal(r.flat, r.flat), [r.r()], [r.r()])
            for c in range(KC):
                o = (f32out if f32out is not None else dst).c(c, lo, hi)
                self.stt(o, self.H.c(c, lo, hi), self.vcol(gcol + c), r.r(), ALU.mult, ALU.mult)

    def proj_chunk(self, slot, mcol, src, nk=KC):
        banks = []
        for t in range(NT):
            b = self.bank()
            banks.append(b)
            for k in range(nk):
                lhsT = (slot.v[:, k, mcol * 128:(mcol + 1) * 128], slot.tk())
                self.mm(self.pv(b), lhsT, src.c(k, t * TT, (t + 1) * TT), k == 0, k == nk - 1)
        return banks

    def proj_tile(self, slot, mcol, src, t, nk=KC):
        b = self.bank()
        for k in range(nk):
            lhsT = (slot.v[:, k, mcol * 128:(mcol + 1) * 128], slot.tk())
            self.mm(self.pv(b), lhsT, src.c(k, t * TT, (t + 1) * TT), k == 0, k == nk - 1)
        return b

    def out_proj(self, dw):
        wv = dw.rearrange("(k p) n -> p k n", p=128)
        sls = [self.load_w(wv[:, :, mt * 512:(mt + 1) * 512], KC, 512) for mt in range(2)]
        for t in range(NT):
            for mt in range(2):
                for mm_ in range(4):
                    b = self.proj_tile(sls[mt], mm_, self.Y, t)
                    h = self.H.c(mt * 4 + mm_, t * TT, (t + 1) * TT)
                    self.tt(h, self.pv(b), h, ALU.add)

    def even_mixer(self, s):
        m = self.mem
        self.wreset()
        wv = self.d_win0.rearrange("(k p) n -> p k n", p=128)
        names = ("XA", "XC", "XCB", "R", "I", "M", "HS", "G")
        sets = []
        for _ in range(2):
            d = {}
            for nm in names:
                if nm == "XA":
                    d[nm] = m.alloc(F32, (SEG + 3,), True)
                elif nm == "XCB":
                    d[nm] = m.alloc(BF16, (SEG,), True)
                else:
                    d[nm] = m.alloc(F32, (SEG,), True)
            sets.append(d)
        csets = [dict(CS=m.alloc(F32, (SEG,), True), CV=m.alloc(F32, (SEG + 2,), True),
                      SC=m.alloc(F32, (SEG,), True)) for _ in range(2)]
        s_xa = self.load_w(wv[:, :, 0:512], KC, 512)
        s_ga = self.load_w(wv[:, :, 512:1024], KC, 512)
        hl = self.H.c(KC - 1, TT, 2 * TT)[1]
        s_c = self.load_w(wv[:, :, 1024:1536], KC, 512, after=hl)
        s_b = self.load_w(wv[:, :, 1536:2048], KC, 512, after=hl)
        for jp in range(2):
            js = (2 * jp, 2 * jp + 1)
            B = {j: sets[j % 2] for j in js}
            bx = {j: [] for j in js}
            for t in range(NT):
                for j in js:
                    bx[j].append(self.proj_tile(s_xa, j, self.XN, t))
            for j in js:
                XA = B[j]["XA"]
                for t in range(NT):
                    self.act(XA.r(3 + t * TT, 3 + (t + 1) * TT), self.pv(bx[j][t]), AF.Identity)
                self.copy(XA.r(0, 3), self.LHALO.c(j))
            bg = {j: self.proj_chunk(s_ga, j, self.XN) for j in js}
            for j in js:
                self_G = B[j]["G"]
                for t in range(NT):
                    self.act(self_G.r(t * TT, (t + 1) * TT), self.pv(bg[j][t]), AF.Gelu_apprx_tanh)
            for j in js:
                XA, XC = B[j]["XA"], B[j]["XC"]
                for k in range(4):
                    w = self.vcol(C_LCW + k * 4 + j)
                    if k == 0:
                        self.ts(XC.r(), XA.r(0, SEG), w, self.vcol(C_LCB + j), ALU.mult, ALU.add)
                    else:
                        self.stt(XC.r(), XA.r(k, k + SEG), w, XC.r(), ALU.mult, ALU.add)
                self.copy(self.LHALO.c(j), XA.r(SEG, SEG + 3))
                self.act(B[j]["XCB"].r(), XC.r(), AF.Identity)
            br, bi = {}, {}
            for j in js:
                XCB = B[j]["XCB"]
                br[j], bi[j] = [], []
                for t in range(NT):
                    b1 = self.bank()
                    self.mm(self.pv(b1), self.WBD.c(j), XCB.r(t * TT, (t + 1) * TT), True, True)
                    b2 = self.bank()
                    self.mm(self.pv(b2), self.WBD.c(4 + j), XCB.r(t * TT, (t + 1) * TT), True, True)
                    br[j].append(b1)
                    bi[j].append(b2)
            for j in js:
                R, I = B[j]["R"], B[j]["I"]
                for t in range(NT):
                    self.act(R.r(t * TT, (t + 1) * TT), self.pv(br[j][t]), AF.Sigmoid, bias=self.vcol(C_BA + j))
                    self.act(I.r(t * TT, (t + 1) * TT), self.pv(bi[j][t]), AF.Sigmoid, bias=self.vcol(C_BX + j))
            for j in js:
                R, M = B[j]["R"], B[j]["M"]
                self.act(M.r(), R.r(), AF.Exp, scale=self.CL2.r(j, j + 1))
                self.act(R.r(), R.r(), AF.Exp, scale=self.CL.r(j, j + 1))
            for j in js:
                M = B[j]["M"]
                self.act(M.r(), M.r(), AF.Sqrt, bias=1.0, scale=-1.0)
            for j in js:
                R, I, M, HS, G, XC = (B[j][k] for k in ("R", "I", "M", "HS", "G", "XC"))
                self.tt(M.r(), M.r(), I.r(), ALU.mult)
                self.tt(M.r(), M.r(), XC.r(), ALU.mult)
                init = self.LST.flat[:, j:j + 1]
                self.op(DVE, lambda e, init=init, HS=HS, R=R, M=M: e.tensor_tensor_scan(
                    HS.flat, R.flat, M.flat, init, ALU.mult, ALU.add), [R.r(), M.r(), self.LST.r()], [HS.r()])
                self.copy(self.LST.r(j, j + 1), HS.r(SEG - 1, SEG))
                self.tt(self.Y.c(j), HS.r(), G.r(), ALU.mult)
        s_v = self.load_w(wv[:, :, 2048:2560], KC, 512)
        for j in range(4):
            CS, CV, SC = (csets[j % 2][k] for k in ("CS", "CV", "SC"))
            bc = self.proj_chunk(s_c, j, self.XN)
            bv = self.proj_chunk(s_v, j, self.XN)
            for t in range(NT):
                self.act(CS.r(t * TT, (t + 1) * TT), self.pv(bc[t]), AF.Identity)
                self.tt(CV.r(2 + t * TT, 2 + (t + 1) * TT), CS.r(t * TT, (t + 1) * TT), self.pv(bv[t]), ALU.mult)
            self.copy(CV.r(0, 2), self.SHALO.c(j))
            for k in range(3):
                w = self.vcol(C_SCW + k * 4 + j)
                if k == 0:
                    self.ts(SC.r(), CV.r(0, SEG), w, self.vcol(C_SCB + j), ALU.mult, ALU.add)
                else:
                    self.stt(SC.r(), CV.r(k, k + SEG), w, SC.r(), ALU.mult, ALU.add)
            self.copy(self.SHALO.c(j), CV.r(SEG, SEG + 2))
            bb = self.proj_chunk(s_b, j, self.XN)
            for t in range(NT):
                self.tt(self.Y.c(4 + j, t * TT, (t + 1) * TT), SC.r(t * TT, (t + 1) * TT), self.pv(bb[t]), ALU.mult)
        self.out_proj(self.d_wout0)

    def ffn(self, l, s):
        m = self.mem
        self.wreset()
        wu = self.d_up[l].rearrange("(k p) n -> p k n", p=128)
        ACTBS = [m.alloc(BF16, (8, SEG), True) for _ in range(2)]
        RAW = [m.alloc(F32, (SEG + 2,), True) for _ in range(4)]
        ACC = [m.alloc(F32, (SEG,), True) for _ in range(4)]
        SG = [m.alloc(F32, (SEG,), True) for _ in range(2)]
        cnt = [0]

        def up(tg):
            kg = tg // 2
            ACTB = ACTBS[kg % 2]
            ncol = 512 if tg < 5 else 256
            sg_ = self.load_w(wu[:, :, tg * 512:tg * 512 + ncol], KC, ncol)
            sv_ = self.load_w(wu[:, :, DFF + tg * 512:DFF + tg * 512 + ncol], KC, ncol)
            pre = {}
            if tg == 0:
                order = [(jj, w) for jj in range(2) for w in range(2)]
                for t in range(NT):
                    for jj, w in order:
                        pre.setdefault((jj, w), []).append(self.proj_tile((sg_, sv_)[w], jj, self.XN, t))
            for jj in range(ncol // 128):
                j = tg * 4 + jj
                jl = j - kg * 8
                accs = []
                for w, (slot, ch) in enumerate(((sg_, j), (sv_, NJ + j))):
                    raw = RAW[cnt[0] % 4]
                    acc = ACC[cnt[0] % 4]
                    cnt[0] += 1
                    banks = pre[(jj, w)] if (jj, w) in pre else self.proj_chunk(slot, jj, self.XN)
                    w0 = self.vcol(C_FCW + l * 132 + 0 * 44 + ch)
                    w1 = self.vcol(C_FCW + l * 132 + 1 * 44 + ch)
                    w2 = self.vcol(C_FCW + l * 132 + 2 * 44 + ch)
                    bb = self.vcol(C_FCB + l * 44 + ch)
                    for t in range(NT):
                        self.act(raw.r(2 + t * TT, 2 + (t + 1) * TT), self.pv(banks[t]), AF.Identity)
                        self.act(acc.r(t * TT, (t + 1) * TT), self.pv(banks[t]), AF.Identity, bias=bb, scale=w2)
                    hal = (self.FHALO.v[:, l, ch, :], self.FHALO.tk())
                    self.copy(raw.r(0, 2), hal)
                    self.stt(acc.r(), raw.r(0, SEG), w0, acc.r(), ALU.mult, ALU.add)
                    self.stt(acc.r(), raw.r(1, SEG + 1), w1, acc.r(), ALU.mult, ALU.add)
                    self.copy(hal, raw.r(SEG, SEG + 2))
                    accs.append(acc)
                sg = SG[j % 2]
                self.act(sg.r(), accs[0].r(), AF.Silu)
                self.tt(ACTB.c(jl), sg.r(), accs[1].r(), ALU.mult)

        def down(kg):
            ACTB = ACTBS[kg % 2]
            njg = 8 if kg < 2 else 6
            if kg == 2:
                sls = []
                for mh in range(2):
                    src = self.d_down[l][kg * 1024:kg * 1024 + njg * 128, mh * 512:(mh + 1) * 512]
                    sls.append(self.load_w(src.rearrange("(k p) n -> p k n", p=128), njg, 512))
                for t in range(NT):
                    for mh in range(2):
                        for mm_ in range(4):
                            b = self.proj_tile(sls[mh], mm_, ACTB, t, nk=njg)
                            h = self.H.c(mh * 4 + mm_, t * TT, (t + 1) * TT)
                            self.tt(h, self.pv(b), h, ALU.add)
                return
            for mh in range(2):
                src = self.d_down[l][kg * 1024:kg * 1024 + njg * 128, mh * 512:(mh + 1) * 512]
                sl = self.load_w(src.rearrange("(k p) n -> p k n", p=128), njg, 512)
                for mm_ in range(4):
                    banks = self.proj_chunk(sl, mm_, ACTB, nk=njg)
                    for t in range(NT):
                        h = self.H.c(mh * 4 + mm_, t * TT, (t + 1) * TT)
                        self.tt(h, self.pv(banks[t]), h, ALU.add)

        up(0)
        up(1)
        up(2)
        down(0)
        up(3)
        up(4)
        down(1)
        up(5)
        down(2)

    def odd_mixer(self, s):
        m = self.mem
        self.wreset()
        wv = self.d_win1.rearrange("(k p) n -> p k n", p=128)
        nblk = SEG // 128
        gb0 = s * nblk
        NB = (s + 1) * nblk
        SPb = m.alloc(F32, (SEG,), True)
        CSEG = m.alloc(F32, (SEG,), True)
        ON8 = m.alloc(F32, (SEG,), True)
        SQ = [m.alloc(F32, (TT,), True) for _ in range(2)]
        SS = m.alloc(F32, (nblk, 8), True)
        GVB = m.alloc(BF16, (nblk, 512), True)
        U = m.alloc(F32, (SEG,), True)
        MX = [m.alloc(F32, (TT,), True) for _ in range(2)]
        KS = m.alloc(BF16, (SEG,), True)
        mark = m.ptr
        KA = m.alloc(BF16, (2, 4 * SEG), True)
        end = m.ptr
        m.ptr = mark
        GE = [m.alloc(F32, (TT,), True) for _ in range(nblk)]
        assert m.ptr <= end
        m.ptr = mark
        VS = m.alloc(BF16, (nblk, 4 * 192), True)
        assert m.ptr <= end
        m.ptr = end
        QA = m.alloc(BF16, (2, SEG), True)
        CHI = m.alloc(BF16, (SEG,), True)
        VA = m.alloc(BF16, (32, 192), True)
        PTb = [m.alloc(BF16, (TT,), True) for _ in range(4)]
        RD = m.alloc(F32, (TT,), True)
        top = m.ptr
        m.ptr = self.work0
        KA2 = m.alloc(BF16, (2, 4 * SEG), True)
        VA2 = m.alloc(BF16, (32, 192), True)
        QA2 = m.alloc(BF16, (2, SEG), True)
        assert m.ptr <= KS.off, (m.ptr, KS.off)
        m.ptr = top
        KAs, VAs, QAs = (KA, KA2), (VA, VA2), (QA, QA2)
        P8 = dict(p0=0, p1=8)

        s_g = self.load_w(wv[:, :, 512:1024], KC, 512)
        s_u = self.load_w(wv[:, :, 0:512], KC, 512)
        for blk in range(nblk):
            b = self.bank()
            for k in range(KC):
                self.mm(self.pv(b), self.XN.c(k, blk * 128, (blk + 1) * 128), (s_g.v[:, k, :], s_g.tk()),
                        k == 0, k == KC - 1)
            ge, sq = GE[blk], SQ[blk % 2]
            self.act(ge.r(), self.pv(b), AF.Gelu_apprx_tanh)
            self.tt(sq.r(), ge.r(), ge.r(), ALU.mult)
            sq3 = sq.flat.rearrange("p (g c) -> p g c", g=8)
            ssb = SS.c(blk)
            self.op(DVE, lambda e, ssb=ssb, sq3=sq3: e.tensor_reduce(ssb[0], sq3, AX.X, ALU.add), [sq.r()], [ssb])
        self.act(SS.r(), SS.r(), AF.Sqrt, bias=EPS, scale=1.0 / 64)
        self.op(DVE, lambda e: e.reciprocal(SS.flat, SS.flat), [SS.r()], [SS.r()])
        for blk in range(nblk):
            ge = GE[blk]
            ge3 = ge.flat.rearrange("p (g c) -> p g c", g=8)
            gv3 = GVB.v[:, blk, :].rearrange("p (g c) -> p g c", g=8)
            bc = SS.v[:, blk, :].unsqueeze(2).to_broadcast([128, 8, 64])
            self.op(DVE, lambda e, gv3=gv3, ge3=ge3, bc=bc: e.tensor_tensor(gv3, ge3, bc, ALU.mult),
                    [ge.r(), SS.r()], [GVB.c(blk)])

        for t in range(NT):
            b = self.bank()
            for k in range(KC):
                self.mm(self.pv(b, 0, TT, 0, 8), (self.WFB.v[:, k, :], self.WFB.tk()),
                        self.XN.c(k, t * TT, (t + 1) * TT), k == 0, k == KC - 1)
            self.act(SPb.r(t * TT, (t + 1) * TT, **P8), self.pv(b, 0, TT, 0, 8), AF.Exp,
                     bias=self.NBF.r(0, 1, **P8), scale=-1.0)
        self.act(SPb.r(0, SEG, **P8), SPb.r(0, SEG, **P8), AF.Ln, bias=1.0)
        self.memset(ON8.r(0, SEG, **P8), 1.0)
        init = self.CCAR.flat[0:8, 0:1]
        self.op(DVE, lambda e: e.tensor_tensor_scan(CSEG.flat[0:8, :], ON8.flat[0:8, :], SPb.flat[0:8, :], init,
                                                    ALU.mult, ALU.subtract),
                [ON8.r(), SPb.r(), self.CCAR.r()], [CSEG.r()])
        self.copy(self.CCAR.r(0, 1, **P8), CSEG.r(SEG - 1, SEG, **P8))
        for cc in range(4):
            bu = self.proj_chunk(s_u, cc, self.XN)
            for t in range(NT):
                self.act(U.r(t * TT, (t + 1) * TT), self.pv(bu[t]), AF.Gelu_apprx_tanh)
            for t in range(NT):
                b = self.bank()
                for bl in range(4):
                    blk = t * 4 + bl
                    for hh in range(2):
                        h = 2 * cc + hh
                        out = (self.PS[hh * 64:(hh + 1) * 64, b, bl * 128:(bl + 1) * 128], [self.ps_tok[b]])
                        lhsT = (GVB.v[:, blk, h * 64:(h + 1) * 64], GVB.c(blk)[1])
                        rhs = (self.WST.v[:, h, :], self.WST.tk())
                        self.mm(out, lhsT, rhs, True, True)
                mx = MX[t % 2]
                mx3 = mx.flat.rearrange("p (a b) -> p a b", a=4)
                ps3 = self.PS[:, b, :].rearrange("p (a b) -> p a b", a=4)
                bs3 = self.BS.v[:, cc, :].unsqueeze(1).to_broadcast([128, 4, 128])
                gn = self.vcol(C_GN + cc)
                self.op(DVE, lambda e, mx3=mx3, ps3=ps3, bs3=bs3, gn=gn: e.scalar_tensor_tensor(
                    mx3, ps3, gn[0], bs3, ALU.mult, ALU.add), [self.pv(b), gn, self.BS.r()], [mx.r()])
                self.tt(self.Y.c(cc, t * TT, (t + 1) * TT), mx.r(), U.r(t * TT, (t + 1) * TT), ALU.mult)

        ident8 = (self.CST.v[0:8, 2, 0:8], self.CST.tk())
        for blk in range(nblk):
            b = self.bank()
            src = CSEG.r(blk * 128, (blk + 1) * 128, **P8)
            out = self.pv(b, 0, 8)
            self.op(PE, lambda e, out=out, src=src: e.transpose(out[0], src[0], ident8[0]), [src, ident8], [out])
            dst = (self.NCT.v[:, gb0 + blk, :], self.NCT.tk())
            self.ts(dst, out, -1.0, None, ALU.mult)

        s_q = self.load_w(wv[:, :, 1024:1536], KC, 512)
        s_k = self.load_w(wv[:, :, 1536:2048], KC, 512)
        s_v = self.load_w(wv[:, :, 2048:2560], KC, 512)
        vs5 = VS.flat.rearrange("p (b c x w) -> p b c x w", b=nblk, c=4, x=3)
        self.memset((vs5[:, :, :, 1, :], VS.tk()), 1.0)
        for blk in range(nblk):
            b = self.bank()
            for k in range(KC):
                self.mm(self.pv(b), self.XN.c(k, blk * 128, (blk + 1) * 128), (s_v.v[:, k, :], s_v.tk()),
                        k == 0, k == KC - 1)
            ps4 = self.PS[:, b, :].rearrange("p (c x w) -> p c x w", c=4, x=2)
            for hh in range(2):
                self.act((vs5[:, blk, :, 2 * hh, :], VS.c(blk)[1]), (ps4[:, :, hh, :], [self.ps_tok[b]]), AF.Identity)
        self.S.burst_begin()
        for cc in range(4):
            vdst = self.d_va[cc][:, gb0:gb0 + nblk, :]
            self.dma(SP, (vdst, [self.tok_va]), (VS.v[:, :, cc * 192:(cc + 1) * 192], VS.tk()), "vst")
        self.S.burst_end()

        self.act(CHI.r(0, SEG, **P8), CSEG.r(0, SEG, **P8), AF.Identity)
        SK = 3
        it = 0

        def prep(cc):
            QA, KA, VA = QAs[cc % 2], KAs[cc % 2], VAs[cc % 2]
            bq = self.proj_chunk(s_q, cc, self.XN)
            for t in range(NT):
                self.act((QA.v[0:64, 0, t * TT:(t + 1) * TT], QA.tk()), self.pv(bq[t], 0, TT, 0, 64),
                         AF.Identity, scale=0.125)
                self.ts((QA.v[0:64, 1, t * TT:(t + 1) * TT], QA.tk()), self.pv(bq[t], 0, TT, 64, 128),
                        0.125, None, ALU.mult)
            bk = self.proj_chunk(s_k, cc, self.XN)
            for t in range(NT):
                self.act(KS.r(t * TT, (t + 1) * TT), self.pv(bk[t]), AF.Identity)
            self.dma(SP, (self.d_kt[cc][:, s * SEG:(s + 1) * SEG], [self.tok_kt[cc]]), KS.r(), "kst")
            key = ("att", cc % 2)
            self.S.burst_begin()
            for hh in range(2):
                h = 2 * cc + hh
                self.dma(SP, (KA.v[0:64, hh, 0:NB * 128], KA.tk()),
                         (self.d_kt[cc][hh * 64:(hh + 1) * 64, 0:NB * 128], [self.tok_kt[cc]]), key)
                self.dma(SP, (QA.v[64:65, hh, :], QA.tk()), (CHI.flat[h:h + 1, :], CHI.tk()), key)
            self.dma(SP, (VA.v[:, 0:NB, :], VA.tk()), (self.d_va[cc][:, 0:NB, :], [self.tok_va]), key)
            self.S.burst_end()
            self.memset((KA.v[64:65, :, 0:NB * 128], KA.tk()), 1.0)

        prep(0)
        for cc in range(4):
            if cc + 1 < 4:
                prep(cc + 1)
            QA, KA, VA = QAs[cc % 2], KAs[cc % 2], VAs[cc % 2]
            steps = []
            for hh in range(2):
                for t in range(NT):
                    Q0 = gb0 + 4 * t
                    nJ = Q0 + 4
                    ob = self.bank(6, 8)
                    for J in range(nJ):
                        steps.append((hh, t, J, Q0, nJ, ob))
            pend = {}
            n = len(steps)
            for i in range(n + SK):
                if i < n:
                    hh, t, J, Q0, nJ, ob = steps[i]
                    h = 2 * cc + hh
                    c0 = max(0, J - Q0) * 128
                    sb_ = self.bank(0, 6)
                    pt = PTb[it % 4]
                    it += 1
                    self.mm(self.pv(sb_, c0, TT), (KA.v[0:65, hh, J * 128:(J + 1) * 128], KA.tk()),
                            (QA.v[0:65, hh, t * TT + c0:(t + 1) * TT], QA.tk()), True, True)
                    bias = (self.NCT.v[:, J, h:h + 1], self.NCT.tk())
                    if J >= Q0:
                        self.mm(self.pv(sb_, c0, c0 + 128), self.MSKB.c(1), self.MSKB.c(0), False, True)
                    self.act(pt.r(c0, TT), self.pv(sb_, c0, TT), AF.Exp, bias=bias)
                    pend[i] = (pt, c0)
                j = i - SK
                if j >= 0:
                    hh, t, J, Q0, nJ, ob = steps[j]
                    pt, c0 = pend.pop(j)
                    self.mm(self.pv(ob, c0, TT), (VA.v[:, J, hh * 64:hh * 64 + 128], VA.tk()), pt.r(c0, TT),
                            J == 0, J == nJ - 1)
                    if J == nJ - 1:
                        p0, p1 = hh * 64, (hh + 1) * 64
                        d0, d1 = (1 - hh) * 64, (2 - hh) * 64
                        den = self.pv(ob, 0, TT, d0, d1)
                        num = self.pv(ob, 0, TT, p0, p1)
                        rd = RD.r(0, TT, p0, p1)
                        self.op(DVE, lambda e, rd=rd, den=den: e.reciprocal(rd[0], den[0]), [den], [rd])
                        self.tt(self.Y.c(4 + cc, t * TT, (t + 1) * TT, p0, p1), num, rd, ALU.mult)
        self.out_proj(self.d_wout1)

    def final(self, s):
        m = self.mem
        self.wreset()
        XO = m.alloc(F32, (8, SEG), True)
        self.rmsnorm(C_NFIN, None, f32out=XO, reset=False)
        for t in range(NT):
            if s + 1 < self.nseg:
                self.hload(s + 1, tiles=[t])
            self.S.burst_begin()
            for c in range(KC):
                dst = self.d_y[c * 128:(c + 1) * 128, s * SEG + t * TT:s * SEG + (t + 1) * TT]
                ev = self.dma(SP, (dst, []), XO.c(c, t * TT, (t + 1) * TT), ("ystore", t))
                self.out_evs.append(ev)
            self.S.burst_end()

    def emit(self, es):
        nc = self.nc
        S = self.S
        csem = {e: es.enter_context(nc.semaphore("s_" + e)) for e in (PE, ACT, DVE, POOL)}
        dsem = {k: es.enter_context(nc.semaphore("d%d" % i)) for i, k in enumerate(S.dma_cnt.keys())}
        for e in ENGS:
            c = 0
            for fn, deps, ev in S.ops[e]:
                if not ev.is_dma and ev.needed:
                    c += 1
                    ev.val = c
            print("engine", e, "ops", len(S.ops[e]), "incs", c, flush=True)
        out_evs = self.out_evs

        def run(name, eng, final=False):
            waited = {}

            def wait(d):
                key = ("d", d.semkey) if d.is_dma else ("c", d.eng)
                if waited.get(key, 0) >= d.val:
                    return
                sem = dsem[d.semkey] if d.is_dma else csem[d.eng]
                eng.wait_ge(sem, d.val)
                waited[key] = d.val

            for fn, deps, ev in S.ops[name]:
                for d in deps:
                    wait(d)
                ins = fn(eng)
                if ev.is_dma:
                    ins.then_inc(dsem[ev.semkey], 16)
                elif ev.needed:
                    ins.then_inc(csem[name], 1)
            if final:
                for d in out_evs:
                    wait(d)

        block = es.enter_context(nc.Block())

        @block.tensor
        def _(e):
            run(PE, e)

        @block.scalar
        def _(e):
            run(ACT, e)

        @block.vector
        def _(e):
            run(DVE, e)

        @block.gpsimd
        def _(e):
            run(POOL, e)

        @block.sync
        def _(e):
            run(SP, e, final=True)


def pack_vecs(inp):
    v = np.zeros((128, NV), np.float32)

    def fm(vec):
        return np.ascontiguousarray(np.asarray(vec, np.float32).reshape(-1, 128).T)

    v[:, C_N0:C_N0 + 8] = fm(inp["mix0_norm"][0])
    v[:, C_N1:C_N1 + 8] = fm(inp["mix1_norm"][0])
    v[:, C_NF0:C_NF0 + 8] = fm(inp["ffn_norm"][0])
    v[:, C_NF1:C_NF1 + 8] = fm(inp["ffn_norm"][1])
    v[:, C_NFIN:C_NFIN + 8] = fm(inp["final_norm"])
    for k in range(4):
        v[:, C_LCW + k * 4:C_LCW + k * 4 + 4] = fm(inp["lru_conv_w"][0, k])
    v[:, C_LCB:C_LCB + 4] = fm(inp["lru_conv_b"][0])
    v[:, C_BA:C_BA + 4] = fm(inp["lru_ba"][0])
    v[:, C_BX:C_BX + 4] = fm(inp["lru_bx"][0])
    v[:, C_LAM:C_LAM + 4] = fm(inp["lru_lambda"][0])
    for k in range(3):
        v[:, C_SCW + k * 4:C_SCW + k * 4 + 4] = fm(inp["sconv_w"][0, k])
    v[:, C_SCB:C_SCB + 4] = fm(inp["sconv_b"][0])
    v[:, C_GN:C_GN + 4] = fm(inp["sgu_norm"][0])
    for l in range(2):
        for k in range(3):
            c0 = C_FCW + l * 132 + k * 44
            v[:, c0:c0 + 44] = fm(inp["ffn_conv_w"][l, k])
        c0 = C_FCB + l * 44
        v[:, c0:c0 + 44] = fm(inp["ffn_conv_b"][l])
    v[0:8, C_BF] = np.asarray(inp["fox_bf"][0], np.float32)
    return v


def host_consts(inp):
    wa = np.asarray(inp["lru_wa"][0], np.float32)
    wx = np.asarray(inp["lru_wx"][0], np.float32)
    wbd = np.zeros((128, 8, 128), np.float32)
    for j in range(4):
        for hh in range(2):
            wbd[hh * 64:(hh + 1) * 64, j, hh * 64:(hh + 1) * 64] = wa[2 * j + hh]
            wbd[hh * 64:(hh + 1) * 64, 4 + j, hh * 64:(hh + 1) * 64] = wx[2 * j + hh]
    sw = np.asarray(inp["sgu_w"][0], np.float32)
    wst = np.ascontiguousarray(sw.transpose(2, 0, 1))
    w1 = np.asarray(inp["mix1_w_in"][0], np.float32)
    wf = np.ascontiguousarray(w1[:, 2560:2568].reshape(8, 128, 8).transpose(1, 0, 2))
    sb = np.asarray(inp["sgu_b"][0], np.float32)
    bs = np.zeros((128, 4, 128), np.float32)
    for cc in range(4):
        bs[0:64, cc, :] = sb[2 * cc][None, :]
        bs[64:128, cc, :] = sb[2 * cc + 1][None, :]
    k = np.arange(128)[:, None]
    q = np.arange(128)[None, :]
    cst = np.zeros((128, 3, 128), np.float32)
    cst[:, 0, :] = np.where(k > q, NEG, 0.0)
    cst[:, 1, :] = (k <= q).astype(np.float32)
    cst[:, 2, :] = np.eye(128, dtype=np.float32)
    sel = np.zeros((8, 8, 128), np.float32)
    for h in range(8):
        sel[h, h, :] = 1.0
    return dict(wbd=wbd, wst=wst, wf=wf, bs=bs, cst=cst, sel=sel, vecs=pack_vecs(inp))


_CACHE = {}


def run_prog(inp, nseg=4, nlayers=2, batches=(0, 1, 2, 3), ncores=8):
    key = (nseg, nlayers)
    if key not in _CACHE:
        p = Prog(nseg=nseg, nlayers=nlayers)
        p.build()
        _CACHE[key] = p
    p = _CACHE[key]
    hc = host_consts(inp)
    x = np.asarray(inp["x"], np.float32)
    T = nseg * SEG
    shared = dict(
        w_in0=np.ascontiguousarray(np.asarray(inp["mix0_w_in"][0], np.float32)),
        w_out0=np.ascontiguousarray(np.asarray(inp["mix0_w_out"][0], np.float32)),
        w_in1=np.ascontiguousarray(np.asarray(inp["mix1_w_in"][0], np.float32)),
        w_out1=np.ascontiguousarray(np.asarray(inp["mix1_w_out"][0], np.float32)),
        w_up=np.ascontiguousarray(np.asarray(inp["ffn_up"], np.float32)),
        w_down=np.ascontiguousarray(np.asarray(inp["ffn_down"], np.float32)),
        **hc,
    )
    stride = 2 if ncores >= 2 * len(batches) else 1
    zeros = None
    maps = []
    for c in range(ncores):
        if c % stride == 0 and c // stride < len(batches):
            m = dict(shared)
            m["xT"] = np.ascontiguousarray(x[batches[c // stride], :T, :].T)
        else:
            if zeros is None:
                zeros = {k: np.zeros_like(v) for k, v in shared.items()}
                zeros["xT"] = np.zeros((D, T), np.float32)
            m = zeros
        maps.append(m)
    res = run_bass_kernel_spmd(p.nc, maps, core_ids=list(range(ncores)))
    outs = [np.ascontiguousarray(res.results[i * stride]["yT"].T) for i in range(len(batches))]
    return outs


def kernel(**inputs):
    outs = run_prog(inputs, nseg=4, nlayers=2)
    return np.stack(outs, axis=0).astype(np.float32)
```
